# Optimizing a Trainium2 kernel written in Bass

```python
import math
import jax, jax.numpy as jnp
from jax import lax
import numpy as np

D_MODEL = 1024
BATCH = 16
SEQ = 4096
DEPTH = 1

D_MIX = D_MODEL
D_SSM = D_MIX // 2
SSM_GROUP = 16
N_SSM_GROUPS = D_SSM // SSM_GROUP
SSM_STATE = 64
D_DN = D_MIX - D_SSM
DN_HEAD_DIM = 128
N_DN_HEADS = D_DN // DN_HEAD_DIM
CONV_K = 5
CHUNK = 64
NORM_EPS = 1e-6
DT_MIN = 1e-3
DT_MAX = 1e-1
IN_COLS = 2 * D_SSM + 4 * D_DN + 4 * N_DN_HEADS

kernel_name = "hybrid_s5_gdn_adaln_block"


def rms_norm(x, w):
    x32 = x.astype(jnp.float32)
    y = x32 * lax.rsqrt(jnp.mean(x32 * x32, axis=-1, keepdims=True) + NORM_EPS)
    return (y * w.astype(jnp.float32)).astype(x.dtype)


def l2_normalize(t):
    return t * lax.rsqrt(jnp.sum(t * t, axis=-1, keepdims=True) + NORM_EPS)


def centred_depthwise_conv(t, w):
    ch = t.shape[-1]
    return lax.conv_general_dilated(
        t, w[:, None, :].astype(t.dtype), window_strides=(1,),
        padding=[(CONV_K // 2, CONV_K // 2)],
        dimension_numbers=("NWC", "WIO", "NWC"), feature_group_count=ch)


def s5_discretize(lam_re, lam_im, log_step, b_re, b_im):
    f32 = jnp.float32
    dt = jnp.exp(log_step.astype(f32))[:, None]
    lr, li = lam_re.astype(f32), lam_im.astype(f32)
    mag = jnp.exp(lr * dt)
    ab_re, ab_im = mag * jnp.cos(li * dt), mag * jnp.sin(li * dt)
    nr, ni = ab_re - 1.0, ab_im
    den = lr * lr + li * li
    f_re = (nr * lr + ni * li) / den
    f_im = (ni * lr - nr * li) / den
    br, bi = b_re.astype(f32), b_im.astype(f32)
    bb_re = f_re[..., None] * br - f_im[..., None] * bi
    bb_im = f_re[..., None] * bi + f_im[..., None] * br
    return ab_re, ab_im, bb_re, bb_im


def s5_direction(u, lam_re, lam_im, log_step, b_re, b_im, c_re, c_im, reverse):
    ab_re, ab_im, bb_re, bb_im = s5_discretize(lam_re, lam_im, log_step, b_re, b_im)
    bu_re = jnp.einsum("blgp,gnp->blgn", u, bb_re)
    bu_im = jnp.einsum("blgp,gnp->blgn", u, bb_im)
    seq = u.shape[1]
    a_re = jnp.broadcast_to(ab_re, (1, seq) + ab_re.shape)
    a_im = jnp.broadcast_to(ab_im, (1, seq) + ab_im.shape)

    def combine(e1, e2):
        a1r, a1i, b1r, b1i = e1
        a2r, a2i, b2r, b2i = e2
        return (a1r * a2r - a1i * a2i,
                a1r * a2i + a1i * a2r,
                a2r * b1r - a2i * b1i + b2r,
                a2r * b1i + a2i * b1r + b2i)

    _, _, xr, xi = lax.associative_scan(combine, (a_re, a_im, bu_re, bu_im),
                                        reverse=reverse, axis=1)
    cr, ci = c_re.astype(jnp.float32), c_im.astype(jnp.float32)
    return jnp.einsum("blgn,gpn->blgp", xr, cr) - jnp.einsum("blgn,gpn->blgp", xi, ci)


def gated_delta_chunked(q, k, v, g, beta):
    bsz, nh, seq, dk = q.shape
    dv = v.shape[-1]
    n = seq // CHUNK
    rs = lambda t: t.reshape((bsz, nh, n, CHUNK) + t.shape[3:])
    q, k, v, g, beta = rs(q), rs(k), rs(v), rs(g), rs(beta)
    g = jnp.cumsum(g, axis=-1)
    idx = jnp.arange(CHUNK)
    lower = idx[:, None] >= idx[None, :]
    strict = idx[:, None] > idx[None, :]
    diff = g[..., :, None] - g[..., None, :]
    decay = jnp.where(lower, jnp.exp(jnp.where(lower, diff, 0.0)), 0.0)
    k_beta = k * beta[..., None]
    lmat = jnp.where(strict, jnp.einsum("bhnid,bhnjd->bhnij", k_beta, k) * decay, 0.0)
    tri = lmat + jnp.eye(CHUNK, dtype=lmat.dtype)
    rhs = jnp.concatenate([v * beta[..., None], k_beta * jnp.exp(g)[..., None]], axis=-1)
    sol = lax.linalg.triangular_solve(tri, rhs, left_side=True, lower=True,
                                      unit_diagonal=True)
    u_c, w_c = sol[..., :dv], sol[..., dv:]
    attn = jnp.where(lower, jnp.einsum("bhnid,bhnjd->bhnij", q, k) * decay, 0.0)
    g_last = g[..., -1]
    k_tail = k * jnp.exp(g_last[..., None] - g)[..., None]
    q_dec = q * jnp.exp(g)[..., None]

    def step(state, inp):
        qd, w, u, a, kt, gl = inp
        v_new = u - jnp.einsum("bhcd,bhde->bhce", w, state)
        o = jnp.einsum("bhcd,bhde->bhce", qd, state) + jnp.einsum("bhij,bhje->bhie", a, v_new)
        state = state * jnp.exp(gl)[..., None, None] + jnp.einsum("bhcd,bhce->bhde", kt, v_new)
        return state, o

    mv = lambda t: jnp.moveaxis(t, 2, 0)
    xs = (mv(q_dec), mv(w_c), mv(u_c), mv(attn), mv(k_tail), mv(g_last))
    s0 = jnp.zeros((bsz, nh, dk, dv), jnp.float32)
    _, o = lax.scan(step, s0, xs)
    return jnp.moveaxis(o, 0, 2).reshape(bsz, nh, seq, dv)


def hybrid_layer(x, c, ada_w, ada_b, norm_w, w_in, conv_w,
                 dn_a_log_f, dn_dt_bias_f, dn_a_log_b, dn_dt_bias_b, dn_norm_w,
                 lam_re_f, lam_im_f, log_step_f, b_re_f, b_im_f, c_re_f, c_im_f,
                 lam_re_b, lam_im_b, log_step_b, b_re_b, b_im_b, c_re_b, c_im_b,
                 s5_d, glu_w, glu_b, w_out):
    f32 = jnp.float32
    dt = x.dtype
    bsz, seq, _ = x.shape
    mod = jax.nn.silu(c) @ ada_w + ada_b
    shift, scale, gate = jnp.split(mod, 3, axis=-1)
    h = rms_norm(x, norm_w) * (1.0 + scale[:, None, :]) + shift[:, None, :]
    proj = h @ w_in
    o1 = D_SSM
    o2 = o1 + D_SSM
    o3 = o2 + 3 * D_DN
    o4 = o3 + D_DN
    o5 = o4 + 2 * N_DN_HEADS
    u_a, z_a, qkv, z_b, beta_fb, alpha_fb = jnp.split(proj, [o1, o2, o3, o4, o5], axis=-1)

    u = u_a.astype(f32).reshape(bsz, seq, N_SSM_GROUPS, SSM_GROUP)
    y = (s5_direction(u, lam_re_f, lam_im_f, log_step_f, b_re_f, b_im_f, c_re_f, c_im_f, False)
         + s5_direction(u, lam_re_b, lam_im_b, log_step_b, b_re_b, b_im_b, c_re_b, c_im_b, True)
         + s5_d.astype(f32).reshape(N_SSM_GROUPS, SSM_GROUP) * u)
    y = jax.nn.gelu(y.reshape(bsz, seq, D_SSM)).astype(dt)
    glu = y @ glu_w + glu_b
    y_a = glu[..., :D_SSM] * jax.nn.sigmoid(glu[..., D_SSM:])
    y_a = y_a * jax.nn.silu(z_a)

    qkv = jax.nn.silu(centred_depthwise_conv(qkv, conv_w))
    q, k, v = jnp.split(qkv, 3, axis=-1)
    to_heads = lambda t: t.reshape(bsz, seq, N_DN_HEADS, DN_HEAD_DIM).transpose(0, 2, 1, 3).astype(f32)
    q = l2_normalize(to_heads(q)) * (DN_HEAD_DIM ** -0.5)
    k = l2_normalize(to_heads(k))
    v = to_heads(v)
    beta = jax.nn.sigmoid(beta_fb.astype(f32)).transpose(0, 2, 1)
    alpha = alpha_fb.astype(f32).transpose(0, 2, 1)
    g_f = -jnp.exp(dn_a_log_f.astype(f32))[None, :, None] * jax.nn.softplus(
        alpha[:, :N_DN_HEADS] + dn_dt_bias_f.astype(f32)[None, :, None])
    g_b = -jnp.exp(dn_a_log_b.astype(f32))[None, :, None] * jax.nn.softplus(
        alpha[:, N_DN_HEADS:] + dn_dt_bias_b.astype(f32)[None, :, None])
    o_f = gated_delta_chunked(q, k, v, g_f, beta[:, :N_DN_HEADS])
    flip = lambda t: jnp.flip(t, axis=2)
    o_b = flip(gated_delta_chunked(flip(q), flip(k), flip(v), flip(g_b),
                                   flip(beta[:, N_DN_HEADS:])))
    o = (o_f + o_b).transpose(0, 2, 1, 3)
    o = o * lax.rsqrt(jnp.mean(o * o, axis=-1, keepdims=True) + NORM_EPS) * dn_norm_w.astype(f32)
    o = o * jax.nn.silu(z_b.astype(f32).reshape(bsz, seq, N_DN_HEADS, DN_HEAD_DIM))
    y_b = o.reshape(bsz, seq, D_DN).astype(dt)

    mix = jnp.concatenate([y_a, y_b], axis=-1) @ w_out
    return x + gate[:, None, :] * mix


def setup_inputs(seed: int = 0) -> dict:
    key = jax.random.key(seed)
    ks = iter(jax.random.split(key, 40))
    f32 = jnp.float32
    nrm = lambda shape, s: jax.random.normal(next(ks), shape, f32) * s
    G, N, P, H = N_SSM_GROUPS, SSM_STATE, SSM_GROUP, N_DN_HEADS

    def s5_params():
        lam_re = -0.5 + nrm((DEPTH, G, N), 0.01)
        lam_im = jnp.pi * jnp.arange(N, dtype=f32)[None, None, :] + nrm((DEPTH, G, N), 0.01)
        log_step = jax.random.uniform(next(ks), (DEPTH, G), f32, math.log(DT_MIN), math.log(DT_MAX))
        b_re = nrm((DEPTH, G, N, P), (2.0 * P) ** -0.5)
        b_im = nrm((DEPTH, G, N, P), (2.0 * P) ** -0.5)
        c_re = nrm((DEPTH, G, P, N), (2.0 * N) ** -0.5)
        c_im = nrm((DEPTH, G, P, N), (2.0 * N) ** -0.5)
        return lam_re, lam_im, log_step, b_re, b_im, c_re, c_im

    def dn_decay_params():
        a_log = jnp.log(jax.random.uniform(next(ks), (DEPTH, H), f32, 1.0, 16.0))
        dts = jnp.exp(jax.random.uniform(next(ks), (DEPTH, H), f32, math.log(DT_MIN), math.log(DT_MAX)))
        dt_bias = dts + jnp.log(-jnp.expm1(-dts))
        return a_log, dt_bias

    x = jax.random.normal(next(ks), (BATCH, SEQ, D_MODEL), f32)
    c = jax.random.normal(next(ks), (BATCH, D_MODEL), f32)
    ada_w = nrm((DEPTH, D_MODEL, 3 * D_MODEL), 0.5 * D_MODEL ** -0.5)
    ada_b = nrm((DEPTH, 3 * D_MODEL), 0.02)
    norm_w = 1.0 + nrm((DEPTH, D_MODEL), 0.02)
    w_in = nrm((DEPTH, D_MODEL, IN_COLS), D_MODEL ** -0.5)
    conv_w = nrm((DEPTH, CONV_K, 3 * D_DN), CONV_K ** -0.5)
    dn_a_log_f, dn_dt_bias_f = dn_decay_params()
    dn_a_log_b, dn_dt_bias_b = dn_decay_params()
    dn_norm_w = 1.0 + nrm((DEPTH, DN_HEAD_DIM), 0.02)
    lam_re_f, lam_im_f, log_step_f, b_re_f, b_im_f, c_re_f, c_im_f = s5_params()
    lam_re_b, lam_im_b, log_step_b, b_re_b, b_im_b, c_re_b, c_im_b = s5_params()
    s5_d = nrm((DEPTH, D_SSM), 1.0)
    glu_w = nrm((DEPTH, D_SSM, 2 * D_SSM), D_SSM ** -0.5)
    glu_b = nrm((DEPTH, 2 * D_SSM), 0.01)
    w_out = nrm((DEPTH, D_MIX, D_MODEL), D_MIX ** -0.5)
    final_norm_w = 1.0 + nrm((D_MODEL,), 0.02)
    return {"x": x, "c": c, "ada_w": ada_w, "ada_b": ada_b, "norm_w": norm_w,
            "w_in": w_in, "conv_w": conv_w,
            "dn_a_log_f": dn_a_log_f, "dn_dt_bias_f": dn_dt_bias_f,
            "dn_a_log_b": dn_a_log_b, "dn_dt_bias_b": dn_dt_bias_b, "dn_norm_w": dn_norm_w,
            "lam_re_f": lam_re_f, "lam_im_f": lam_im_f, "log_step_f": log_step_f,
            "b_re_f": b_re_f, "b_im_f": b_im_f, "c_re_f": c_re_f, "c_im_f": c_im_f,
            "lam_re_b": lam_re_b, "lam_im_b": lam_im_b, "log_step_b": log_step_b,
            "b_re_b": b_re_b, "b_im_b": b_im_b, "c_re_b": c_re_b, "c_im_b": c_im_b,
            "s5_d": s5_d, "glu_w": glu_w, "glu_b": glu_b, "w_out": w_out,
            "final_norm_w": final_norm_w}


def reference(x, c, ada_w, ada_b, norm_w, w_in, conv_w,
              dn_a_log_f, dn_dt_bias_f, dn_a_log_b, dn_dt_bias_b, dn_norm_w,
              lam_re_f, lam_im_f, log_step_f, b_re_f, b_im_f, c_re_f, c_im_f,
              lam_re_b, lam_im_b, log_step_b, b_re_b, b_im_b, c_re_b, c_im_b,
              s5_d, glu_w, glu_b, w_out, final_norm_w):
    for layer in range(DEPTH):
        x = hybrid_layer(
            x, c, ada_w[layer], ada_b[layer], norm_w[layer], w_in[layer], conv_w[layer],
            dn_a_log_f[layer], dn_dt_bias_f[layer], dn_a_log_b[layer], dn_dt_bias_b[layer],
            dn_norm_w[layer],
            lam_re_f[layer], lam_im_f[layer], log_step_f[layer], b_re_f[layer], b_im_f[layer],
            c_re_f[layer], c_im_f[layer],
            lam_re_b[layer], lam_im_b[layer], log_step_b[layer], b_re_b[layer], b_im_b[layer],
            c_re_b[layer], c_im_b[layer],
            s5_d[layer], glu_w[layer], glu_b[layer], w_out[layer])
    return rms_norm(x, final_norm_w)
```

```python
import math
import numpy as np
from contextlib import ExitStack
import concourse.bass as bass
import concourse.mybir as mybir
from concourse.bass_utils import run_bass_kernel_spmd

F32 = mybir.dt.float32
BF16 = mybir.dt.bfloat16
AF = mybir.ActivationFunctionType
ALU = mybir.AluOpType

D = 1024
NCORES = 8
EPS = 1e-6
MAGIC = 12582912.0
TWO_PI_S = 6.28318
NEGBIG = -30000.0
IN_COLS = 3088


class Sched:
    def __init__(self, nc, es):
        self.nc = nc
        self.es = es
        self.engs = {'pe': nc.tensor, 'act': nc.scalar, 'dve': nc.vector, 'pool': nc.gpsimd, 'sp': nc.sync}
        self.sem = {k: es.enter_context(nc.semaphore('s_' + k)) for k in ['pe', 'act', 'dve', 'pool']}
        self.cnt = {k: 0 for k in self.sem}
        self.waited = {k: {} for k in self.engs}
        self.lw = {}
        self.rd = {}
        self.dsem = {}
        self.nwaits = 0
        self.dead = False
        self.stop_at = None

    def _handle(self, k):
        return self.sem[k] if k in self.sem else self.dsem[k][0]

    def _waits(self, e, reads, writes):
        hard = {}
        soft = {}

        def add(d, ev):
            if ev is not None and d.get(ev[0], 0) < ev[1]:
                d[ev[0]] = ev[1]
        for r in reads:
            add(hard, self.lw.get(r))
        for w in writes:
            add(hard, self.lw.get(w))
            for k, v in self.rd.get(w, {}).items():
                add(soft, (k, v))
        for k, v in soft.items():
            add(hard, (k, v))
        eng = self.engs[e]
        for k, v in hard.items():
            if k == e and e == 'pe':
                continue
            if self.waited[e].get(k, 0) < v:
                eng.wait_ge(self._handle(k), v)
                self.waited[e][k] = v
                self.nwaits += 1

    def mark(self, name):
        if self.stop_at == name and not self.dead:
            self.barrier()
            self.dead = True

    def op(self, e, fn, r=(), w=()):
        if self.dead:
            return
        self._waits(e, r, w)
        ins = fn(self.engs[e])
        self.cnt[e] += 1
        ins.then_inc(self.sem[e], 1)
        ev = (e, self.cnt[e])
        for x in w:
            self.lw[x] = ev
            self.rd[x] = {}
        for x in r:
            self.rd.setdefault(x, {})[e] = self.cnt[e]

    def dma(self, q, out, in_, r=(), w=(), key=None, **kw):
        if self.dead:
            return
        self._waits(q, r, w)
        if key is None:
            key = w[0] if w else r[0]
        sk = ('d', key)
        if sk not in self.dsem:
            self.dsem[sk] = [self.es.enter_context(self.nc.semaphore('d%d' % len(self.dsem))), 0]
        self.dsem[sk][1] += 16
        self.engs[q].dma_start(out=out, in_=in_, **kw).then_inc(self.dsem[sk][0], 16)
        ev = (sk, self.dsem[sk][1])
        for x in w:
            self.lw[x] = ev
            self.rd[x] = {}
        for x in r:
            self.rd.setdefault(x, {})[sk] = self.dsem[sk][1]

    def barrier(self):
        if self.dead:
            return
        evs = {k: v for k, v in self.cnt.items() if v > 0}
        for sk, (h, c) in self.dsem.items():
            if c > 0:
                evs[sk] = c
        for e in self.engs:
            for k, v in evs.items():
                if k == e:
                    continue
                if self.waited[e].get(k, 0) < v:
                    self.engs[e].wait_ge(self._handle(k), v)
                    self.waited[e][k] = v

    def mm(self, out, lhsT, rhs, start=True, stop=True, r=(), w=()):
        self.op('pe', lambda e: e.matmul(out, lhsT=lhsT, rhs=rhs, start=start, stop=stop), r, w)

    def tr(self, out, in_, ident, r=(), w=()):
        self.op('pe', lambda e: e.transpose(out=out, in_=in_, identity=ident), r, w)

    def act(self, out, in_, func, bias=None, scale=None, accum=None, r=(), w=()):
        kw = {}
        if bias is not None:
            kw['bias'] = bias
        if scale is not None:
            kw['scale'] = scale
        if accum is not None:
            kw['accum_out'] = accum
        self.op('act', lambda e: e.activation(out=out, in_=in_, func=func, **kw), r, w)

    def tt(self, eng, out, in0, in1, op, r=(), w=()):
        self.op(eng, lambda e: e.tensor_tensor(out=out, in0=in0, in1=in1, op=op), r, w)

    def ts(self, eng, out, in0, s1, s2=None, op0=ALU.mult, op1=None, r=(), w=()):
        if op1 is None:
            self.op(eng, lambda e: e.tensor_scalar(out=out, in0=in0, scalar1=s1, scalar2=None, op0=op0), r, w)
        else:
            self.op(eng, lambda e: e.tensor_scalar(out=out, in0=in0, scalar1=s1, scalar2=s2, op0=op0, op1=op1), r, w)

    def stt(self, out, in0, scalar, in1, op0, op1, r=(), w=()):
        self.op('dve', lambda e: e.scalar_tensor_tensor(out=out, in0=in0, scalar=scalar, in1=in1, op0=op0, op1=op1), r, w)

    def copy(self, eng, out, in_, r=(), w=()):
        if eng == 'act':
            self.op('act', lambda e: e.copy(out=out, in_=in_), r, w)
        else:
            self.op(eng, lambda e: e.tensor_copy(out=out, in_=in_), r, w)

    def memset(self, eng, ap, val, w=()):
        self.op(eng, lambda e: e.memset(ap, val), (), w)

    def asel(self, out, in_, pattern, cmp, fill, base, cm, r=(), w=()):
        self.op('pool', lambda e: e.affine_select(out=out, in_=in_, pattern=pattern, compare_op=cmp, fill=fill,
                                                  base=base, channel_multiplier=cm), r, w)


class Pool:
    def __init__(self, S, ctx, name, shape, dt, n, space='sbuf'):
        self.S = S
        self.name = name
        self.n = n
        self.i = 0
        alloc = S.nc.sbuf_tensor if space == 'sbuf' else S.nc.psum_tensor
        self.t = [ctx.enter_context(alloc('sbp_%s_%d' % (name, k), shape, dt)) for k in range(n)]

    def next(self):
        k = self.i % self.n
        self.i += 1
        key = (self.name, k)
        if key in self.S.lw and not self.S.rd.get(key) and not self.S.dead:
            raise RuntimeError('pool slot %s reused before being read' % (key,))
        return self.t[k], key


def build_nc(L, NB, dbg=False, stop_at=None):
    assert L % 1024 == 0
    NT = L // 128
    NCK = L // 8
    NTB = L // 1024
    NB5 = L // 512
    NLV = int(round(math.log2(NCK)))
    nc = bass.Bass("TRN2", target_bir_lowering=False)

    def din(name, shape, dt=F32):
        return nc.dram_tensor(name, shape, dt, kind="ExternalInput").ap()
    x_d = din("x", [NB, L, D])
    cT_d = din("cT", [128, 8, NB])
    adaw_d = din("ada_w", [D, 3 * D])
    adabT_d = din("ada_bT", [128, 24])
    adabg_d = din("ada_bg", [1, D])
    normwT_d = din("norm_wT", [128, 8])
    win_d = din("w_in", [D, IN_COLS])
    convw_d = din("conv_wT", [128, 12, 5])
    alog_d = din("dn_alog", [1, 8])
    dtb_d = din("dn_dtb", [1, 8])
    dnw_d = din("dn_norm_w", [128, 1])
    lamre_d = din("lamre", [128, 32])
    lamim_d = din("lamim", [128, 32])
    lstep_d = din("lstep", [128, 32])
    Bre_d = din("Bre", [128, 32, 16])
    Bim_d = din("Bim", [128, 32, 16])
    CTre_d = din("CTre", [128, 32, 16])
    CTim_d = din("CTim", [128, 32, 16])
    Dcol_d = din("Dcol", [128, 32])
    gluw_d = din("glu_w", [512, 1024])
    glubT_d = din("glu_bT", [128, 8])
    wout_d = din("w_out", [D, D])
    fnw_d = din("fnw", [1, D])
    out_d = nc.dram_tensor("out", [NB, L, D], F32, kind="ExternalOutput").ap()
    skind = "ExternalOutput" if dbg else "Internal"
    ycat_d = nc.dram_tensor("ycat", [NB, D, L], BF16, kind=skind).ap()
    tabK_d = nc.dram_tensor("tabK", [128, 32 * 128], BF16, kind="Internal").ap()
    tabW_d = nc.dram_tensor("tabW", [128, 32 * 256], BF16, kind="Internal").ap()
    tabM_d = nc.dram_tensor("tabM", [128, 16 * 512], BF16, kind="Internal").ap()
    if dbg:
        hTdbg_d = nc.dram_tensor("hTdbg", [NB, 128, 8, L], BF16, kind="ExternalOutput").ap()

    with ExitStack() as es:
        S = Sched(nc, es)
        S.stop_at = stop_at

        def T(name, shape, dt=F32, ctx=es):
            return ctx.enter_context(nc.sbuf_tensor('sb_' + name, shape, dt))

        psF_t = [es.enter_context(nc.psum_tensor('psF%d' % i, [128, 512], F32)) for i in range(6)]
        psB_t = [es.enter_context(nc.psum_tensor('psB%d' % i, [128, 1024], BF16)) for i in range(2)]

        class PS:
            def __init__(self):
                self.iF = 0
                self.iQ = 0
                self.iB = 0
                self.iBq = 0
                self.fullbanks = [0, 1, 2, 3, 4, 5]
                self.qbanks = []

            def cfg(self, full, q):
                self.fullbanks = full
                self.qbanks = q
                self.iF = 0
                self.iQ = 0

            def full(self):
                b = self.fullbanks[self.iF % len(self.fullbanks)]
                self.iF += 1
                return psF_t[b], ('psF', b)

            def quarter(self):
                nb_ = len(self.qbanks)
                k = self.iQ % (nb_ * 4)
                self.iQ += 1
                b = self.qbanks[k % nb_]
                q = k // nb_
                return psF_t[b][:, q * 128:(q + 1) * 128], ('psF', b)

            def bfull(self):
                b = self.iB % 2
                self.iB += 1
                return psB_t[b], ('psB', b)

            def bq(self):
                k = self.iBq % 16
                self.iBq += 1
                return psB_t[k % 2][:, (k // 2) * 128:(k // 2 + 1) * 128], ('psB', k % 2)
        ps = PS()

        ones_f = T('ones_f', [128, 128])
        ident_f = T('ident_f', [128, 128])
        ident_b = T('ident_b', [128, 128], BF16)
        ones_b = T('ones_b', [128, 128], BF16)
        triF = T('triF', [128, 128])
        triB = T('triB', [128, 128])
        S.memset('pool', ones_f[:], 1.0, w=['ones_f'])
        S.asel(ident_f[:], ones_f[:], [[-1, 128]], ALU.is_equal, 0.0, 0, 1, r=['ones_f'], w=['ident_f'])
        S.copy('pool', ident_b[:], ident_f[:], r=['ident_f'], w=['ident_b'])
        S.copy('pool', ones_b[:], ones_f[:], r=['ones_f'], w=['ones_b'])
        S.asel(triF[:], ones_f[:], [[1, 128]], ALU.is_ge, 0.0, 0, -1, r=['ones_f'], w=['triF'])
        S.asel(triB[:], ones_f[:], [[-1, 128]], ALU.is_ge, 0.0, 0, 1, r=['ones_f'], w=['triB'])
        lmask = T('lmask', [128, 7, 128])
        bsx = ExitStack()
        Bs = T('Bs', [128, 8, 128], F32, ctx=bsx)
        for li in range(8):
            s_ = 1 << li
            nb_ = 128 // s_
            if s_ == 128:
                S.copy('pool', Bs[:, li, :], ones_f[:], r=['ones_f'], w=['Bs'])
            else:
                S.asel(Bs[:, li, :].rearrange("p (b r) -> p b r", r=s_), ones_f[:].rearrange("p (b r) -> p b r", r=s_),
                       [[-s_, nb_], [0, s_]], ALU.is_ge, 0.0, 0, 1, r=['ones_f'], w=['Bs'])
                S.asel(Bs[:, li, :].rearrange("p (b r) -> p b r", r=s_), Bs[:, li, :].rearrange("p (b r) -> p b r", r=s_),
                       [[s_, nb_], [0, s_]], ALU.is_ge, 0.0, s_ - 1, -1, r=['Bs'], w=['Bs'])
        for li in range(7):
            S.tt('pool', lmask[:, li, :], Bs[:, li + 1, :], Bs[:, li, :], ALU.subtract, r=['Bs'], w=['lmask'])
        S.barrier()
        bsx.close()
        APre = T('APre', [128, 32, 10])
        APim = T('APim', [128, 32, 10])
        nAPim = T('nAPim', [128, 32, 10])
        A_sc = T('A_sc', [128, 8, NB])
        shiftT = T('shiftT', [128, 8, NB])
        gate_row = T('gate_row', [128, NB, D])
        fnw_row = T('fnw_row', [128, D])
        glubT = T('glubT', [128, 8])
        convw = T('convw', [128, 12, 5])
        dnw = T('dnw', [128, 1])
        alog_b = T('alog_b', [128, 8])
        dtb_b = T('dtb_b', [128, 8])
        nexpa = T('nexpa', [128, 8])
        S.dma('sp', fnw_row[:], fnw_d[0:1, :].to_broadcast([128, D]), w=['fnw_row'])
        S.dma('sp', glubT[:], glubT_d[:, :], w=['glubT'])
        S.dma('sp', convw[:], convw_d[:, :, :], w=['convw'])
        S.dma('sp', dnw[:], dnw_d[:, :], w=['dnw'])
        S.dma('sp', alog_b[:], alog_d[0:1, :].to_broadcast([128, 8]), w=['alog_b'])
        S.dma('sp', dtb_b[:], dtb_d[0:1, :].to_broadcast([128, 8]), w=['dtb_b'])
        S.act(nexpa[:], alog_b[:], AF.Exp, r=['alog_b'], w=['nexpa'])
        S.ts('dve', nexpa[:], nexpa[:], -1.0, r=['nexpa'], w=['nexpa'])
        S.mark('t0')

        with ExitStack() as tsx:
            def TT(name, shape, dt=F32):
                return T(name, shape, dt, ctx=tsx)
            lamre = TT('lamre', [128, 32]); lamim = TT('lamim', [128, 32]); lstep = TT('lstep', [128, 32])
            Bre = TT('Bre', [128, 32, 16]); Bim = TT('Bim', [128, 32, 16])
            CTre = TT('CTre', [128, 32, 16]); CTim = TT('CTim', [128, 32, 16])
            Dcol = TT('Dcol', [128, 32])
            for nm, t, d_ in [('lamre', lamre, lamre_d), ('lamim', lamim, lamim_d), ('lstep', lstep, lstep_d),
                              ('Dcol', Dcol, Dcol_d)]:
                S.dma('sp', t[:], d_[:, :], w=[nm])
            for nm, t, d_ in [('Bre', Bre, Bre_d), ('Bim', Bim, Bim_d), ('CTre', CTre, CTre_d), ('CTim', CTim, CTim_d)]:
                S.dma('sp', t[:], d_[:, :, :], w=[nm])
            kv = TT('kv', [128, 1, 40])
            kvals = [-s for s in range(8)] + [7 - s for s in range(8)] + [s for s in range(8)] + \
                    [s + 1 for s in range(8)] + [8 - s for s in range(8)]
            for s_, k_ in enumerate(kvals):
                S.memset('pool', kv[:, :, s_:s_ + 1], float(k_), w=['kv'])
            SL_NEG, SL_REV7, SL_POS, SL_POS1, SL_REV8 = 0, 8, 16, 24, 32
            dtt = TT('dtt', [128, 32]); lrd = TT('lrd', [128, 32]); lid = TT('lid', [128, 32])
            S.act(dtt[:], lstep[:], AF.Exp, r=['lstep'], w=['dtt'])
            S.tt('dve', lrd[:], lamre[:], dtt[:], ALU.mult, r=['lamre', 'dtt'], w=['lrd'])
            S.tt('dve', lid[:], lamim[:], dtt[:], ALU.mult, r=['lamim', 'dtt'], w=['lid'])
            kvb = kv[:].to_broadcast([128, 32, 40])
            PWre = TT('PWre', [128, 32, 40]); PWim = TT('PWim', [128, 32, 40])
            pwx = ExitStack()
            w1 = T('w1', [128, 32, 40], F32, ctx=pwx); w2 = T('w2', [128, 32, 40], F32, ctx=pwx); w3 = T('w3', [128, 32, 40], F32, ctx=pwx)
            mag = T('mag', [128, 32, 40], F32, ctx=pwx)
            lrdb = lrd[:].unsqueeze(2).to_broadcast([128, 32, 40])
            lidb = lid[:].unsqueeze(2).to_broadcast([128, 32, 40])
            S.tt('dve', w1[:], lrdb, kvb, ALU.mult, r=['lrd', 'kv'], w=['w1'])
            S.act(mag[:], w1[:], AF.Exp, r=['w1'], w=['mag'])
            S.stt(w1[:], lidb, 1.0 / (2.0 * math.pi), kvb, ALU.mult, ALU.mult, r=['lid', 'kv', 'mag'], w=['w1'])
            S.ts('dve', w2[:], w1[:], MAGIC, MAGIC, ALU.add, ALU.subtract, r=['w1'], w=['w2'])
            S.tt('dve', w2[:], w1[:], w2[:], ALU.subtract, r=['w1', 'w2'], w=['w2'])
            S.act(w3[:], w2[:], AF.Sin, scale=TWO_PI_S, r=['w2'], w=['w3'])
            S.tt('dve', PWim[:], mag[:], w3[:], ALU.mult, r=['mag', 'w3'], w=['PWim'])
            S.ts('dve', w1[:], w1[:], 0.25, None, ALU.add, None, r=['w1'], w=['w1'])
            S.ts('dve', w2[:], w1[:], MAGIC, MAGIC, ALU.add, ALU.subtract, r=['w1'], w=['w2'])
            S.tt('dve', w2[:], w1[:], w2[:], ALU.subtract, r=['w1', 'w2'], w=['w2'])
            S.act(w3[:], w2[:], AF.Sin, scale=TWO_PI_S, r=['w2'], w=['w3'])
            S.tt('dve', PWre[:], mag[:], w3[:], ALU.mult, r=['mag', 'w3'], w=['PWre'])
            S.mark('t1')
            S.barrier()
            pwx.close()
            a1re = PWre[:, :, SL_POS + 1]; a1im = PWim[:, :, SL_POS + 1]
            nr = TT('nr', [128, 32]); den = TT('den', [128, 32]); q1 = TT('q1', [128, 32]); q2 = TT('q2', [128, 32])
            fre = TT('fre', [128, 32]); fim = TT('fim', [128, 32])
            S.ts('dve', nr[:], a1re, -1.0, None, ALU.add, None, r=['PWre'], w=['nr'])
            S.tt('dve', den[:], lamre[:], lamre[:], ALU.mult, r=['lamre'], w=['den'])
            S.tt('dve', q1[:], lamim[:], lamim[:], ALU.mult, r=['lamim'], w=['q1'])
            S.tt('dve', den[:], den[:], q1[:], ALU.add, r=['den', 'q1'], w=['den'])
            S.op('dve', lambda e: e.reciprocal(out=den[:], in_=den[:]), r=['den'], w=['den'])
            S.tt('dve', q1[:], nr[:], lamre[:], ALU.mult, r=['nr', 'lamre', 'den'], w=['q1'])
            S.tt('dve', q2[:], a1im, lamim[:], ALU.mult, r=['PWim', 'lamim'], w=['q2'])
            S.tt('dve', q1[:], q1[:], q2[:], ALU.add, r=['q1', 'q2'], w=['q1'])
            S.tt('dve', fre[:], q1[:], den[:], ALU.mult, r=['q1', 'den'], w=['fre'])
            S.tt('dve', q1[:], a1im, lamre[:], ALU.mult, r=['PWim', 'lamre', 'fre'], w=['q1'])
            S.tt('dve', q2[:], nr[:], lamim[:], ALU.mult, r=['nr', 'lamim', 'q1'], w=['q2'])
            S.tt('dve', q1[:], q1[:], q2[:], ALU.subtract, r=['q1', 'q2'], w=['q1'])
            S.tt('dve', fim[:], q1[:], den[:], ALU.mult, r=['q1', 'den'], w=['fim'])
            BBre = TT('BBre', [128, 32, 16]); BBim = TT('BBim', [128, 32, 16])
            u1 = TT('u1', [128, 32, 16]); u2 = TT('u2', [128, 32, 16])
            freb = fre[:].unsqueeze(2).to_broadcast([128, 32, 16]); fimb = fim[:].unsqueeze(2).to_broadcast([128, 32, 16])
            S.tt('dve', u1[:], Bre[:], freb, ALU.mult, r=['Bre', 'fre'], w=['u1'])
            S.tt('dve', u2[:], Bim[:], fimb, ALU.mult, r=['Bim', 'fim'], w=['u2'])
            S.tt('dve', BBre[:], u1[:], u2[:], ALU.subtract, r=['u1', 'u2'], w=['BBre'])
            S.tt('dve', u1[:], Bim[:], freb, ALU.mult, r=['Bim', 'fre', 'BBre'], w=['u1'])
            S.tt('dve', u2[:], Bre[:], fimb, ALU.mult, r=['Bre', 'fim', 'BBre'], w=['u2'])
            S.tt('dve', BBim[:], u1[:], u2[:], ALU.add, r=['u1', 'u2'], w=['BBim'])
            S.copy('dve', APre[:, :, 0], PWre[:, :, SL_POS1 + 7], r=['PWre'], w=['APre'])
            S.copy('dve', APim[:, :, 0], PWim[:, :, SL_POS1 + 7], r=['PWim'], w=['APim'])
            for k in range(9):
                S.tt('dve', q1[:], APre[:, :, k], APre[:, :, k], ALU.mult, r=['APre', 'q1'], w=['q1'])
                S.tt('dve', q2[:], APim[:, :, k], APim[:, :, k], ALU.mult, r=['APim', 'q2'], w=['q2'])
                S.tt('dve', APre[:, :, k + 1], q1[:], q2[:], ALU.subtract, r=['q1', 'q2'], w=['APre'])
                S.stt(APim[:, :, k + 1], APre[:, :, k], 2.0, APim[:, :, k], ALU.mult, ALU.mult, r=['APre', 'APim'], w=['APim'])
            S.ts('dve', nAPim[:], APim[:], -1.0, r=['APim'], w=['nAPim'])
            S.mark('t2')

            tabK = TT('tabK', [128, 32, 128], BF16)
            tabW = TT('tabW', [128, 32, 2, 2, 64], BF16)
            tabM = TT('tabM', [128, 16, 2, 2, 128], BF16)
            Kacc = TT('Kacc', [128, 32, 128])
            mKf = TT('mKf', [128, 8, 16]); mKb = TT('mKb', [128, 8, 16])
            S.asel(mKf[:], ones_f[:].rearrange("p (j q) -> p j q", q=16), [[16, 8], [0, 16]], ALU.is_ge, 0.0, 15, -1,
                   r=['ones_f'], w=['mKf'])
            S.asel(mKb[:], ones_f[:].rearrange("p (j q) -> p j q", q=16), [[-16, 8], [0, 16]], ALU.is_ge, 0.0, 0, 1,
                   r=['ones_f'], w=['mKb'])
            S.mark('t3')
            Gre = TT('Gre', [128, 16, 8, 16]); Gim = TT('Gim', [128, 16, 8, 16])
            G7re = TT('G7re', [128, 16, 8, 16]); G7im = TT('G7im', [128, 16, 8, 16])
            Hre = TT('Hre', [128, 16, 8, 16]); Hnim = TT('Hnim', [128, 16, 8, 16])
            c1 = TT('c1', [128, 16, 8, 16]); c2 = TT('c2', [128, 16, 8, 16])
            Hbre = TT('Hbre', [128, 16, 2, 128]); Hbnim = TT('Hbnim', [128, 16, 2, 128])
            mhalf = TT('mhalf', [128, 2])
            S.asel(mhalf[:, 0:1], ones_f[:, 0:1], [[0, 1]], ALU.is_ge, 0.0, 63, -1, r=['ones_f'], w=['mhalf'])
            S.asel(mhalf[:, 1:2], ones_f[:, 0:1], [[0, 1]], ALU.is_ge, 0.0, -64, 1, r=['ones_f'], w=['mhalf'])

            def cmul_outer(dre, dim_, kre, kim, slot0, d, Tre_, Tim_, tre_k, tim_k, neg_im=False):
                c0 = d * 16
                pr = PWre[:, c0:c0 + 16, slot0:slot0 + 8].unsqueeze(3).to_broadcast([128, 16, 8, 16])
                pi = PWim[:, c0:c0 + 16, slot0:slot0 + 8].unsqueeze(3).to_broadcast([128, 16, 8, 16])
                tr_ = Tre_[:, c0:c0 + 16, :].unsqueeze(2).to_broadcast([128, 16, 8, 16])
                ti_ = Tim_[:, c0:c0 + 16, :].unsqueeze(2).to_broadcast([128, 16, 8, 16])
                S.tt('dve', c1[:], pr, tr_, ALU.mult, r=['PWre', tre_k], w=['c1', ('c1', 0), ('c1', 1)])
                S.tt('pool', c2[:], pi, ti_, ALU.mult, r=['PWim', tim_k], w=['c2'])
                S.tt('dve', dre, c1[:], c2[:], ALU.subtract, r=['c1', 'c2'], w=[kre])
                S.tt('dve', c1[:], pr, ti_, ALU.mult, r=['PWre', tim_k], w=['c1', ('c1', 0), ('c1', 1)])
                S.tt('pool', c2[:], pi, tr_, ALU.mult, r=['PWim', tre_k], w=['c2'])
                if neg_im:
                    S.stt(dim_, c1[:], -1.0, c2[:], ALU.mult, ALU.subtract, r=['c1', 'c2'], w=[kim])
                else:
                    S.tt('dve', dim_, c1[:], c2[:], ALU.add, r=['c1', 'c2'], w=[kim])

            ps.cfg([0, 1, 2, 3, 4, 5], [])
            for d in range(2):
                if d == 0:
                    cmul_outer(Gre[:], Gim[:], 'Gre', 'Gim', SL_NEG, 0, BBre, BBim, 'BBre', 'BBim')
                    S.mark('u1')
                    cmul_outer(G7re[:], G7im[:], 'G7re', 'G7im', SL_REV7, 0, BBre, BBim, 'BBre', 'BBim')
                    cmul_outer(Hre[:], Hnim[:], 'Hre', 'Hnim', SL_POS, 0, CTre, CTim, 'CTre', 'CTim', neg_im=True)
                    wre, wim, wkr, wki = G7re, G7im, 'G7re', 'G7im'
                    mslot = SL_POS1
                else:
                    cmul_outer(Gre[:], Gim[:], 'Gre', 'Gim', SL_POS, 1, BBre, BBim, 'BBre', 'BBim')
                    cmul_outer(Hre[:], Hnim[:], 'Hre', 'Hnim', SL_NEG, 1, CTre, CTim, 'CTre', 'CTim', neg_im=True)
                    wre, wim, wkr, wki = Gre, Gim, 'Gre', 'Gim'
                    mslot = SL_REV8
                cmul_outer(tabM[:, :, d, 0, :].rearrange("p g (j q) -> p g j q", q=16),
                           tabM[:, :, d, 1, :].rearrange("p g (j q) -> p g j q", q=16),
                           'tabM', 'tabM', mslot, d, CTre, CTim, 'CTre', 'CTim', neg_im=True)
                S.mark('u2')
                for two in range(2):
                    S.ts('dve', Hbre[:, :, two, :], Hre[:].rearrange("p g j q -> p g (j q)"), mhalf[:, two:two + 1], r=['Hre', 'mhalf'], w=['Hbre'])
                    S.ts('dve', Hbnim[:, :, two, :], Hnim[:].rearrange("p g j q -> p g (j q)"), mhalf[:, two:two + 1], r=['Hnim', 'mhalf'], w=['Hbnim'])
                for gp in range(16):
                    pk, pkk = ps.full()
                    S.mm(pk[:, 0:256], lhsT=Gre[:, gp, :, :], rhs=Hbre[:, gp, :, :], start=True, stop=False, r=['Gre', 'Hbre'], w=[pkk])
                    S.mm(pk[:, 0:256], lhsT=Gim[:, gp, :, :], rhs=Hbnim[:, gp, :, :], start=False, stop=True, r=['Gim', 'Hbnim'], w=[pkk])
                    for two in range(2):
                        g = 2 * gp + two
                        pkq = pk[:, two * 128:(two + 1) * 128]
                        if d == 0:
                            S.tt('dve', Kacc[:, g, :], pkq, mKf[:].rearrange("p j q -> p (j q)"), ALU.mult,
                                 r=[pkk, 'mKf'], w=[('Kacc', g)])
                        else:
                            S.tt('dve', c1[:, two, :, :].rearrange("p j q -> p (j q)"), pkq, mKb[:].rearrange("p j q -> p (j q)"),
                                 ALU.mult, r=[pkk, 'mKb'], w=[('c1', two)])
                            S.tt('pool', Kacc[:, g, :], Kacc[:, g, :], c1[:, two, :, :].rearrange("p j q -> p (j q)"), ALU.add,
                                 r=[('Kacc', g), ('c1', two)], w=[('Kacc', g)])
                            S.stt(tabK[:, g, :], ident_f[:], Dcol[:, g:g + 1], Kacc[:, g, :], ALU.mult, ALU.add,
                                  r=['ident_f', 'Dcol', ('Kacc', g)], w=['tabK'])
                    pw, pwk = ps.full()
                    S.tr(pw[:, 0:128], wre[:, gp, :, :], ident_f[:], r=[wkr, 'ident_f'], w=[pwk])
                    S.tr(pw[:, 128:256], wim[:, gp, :, :], ident_f[:], r=[wki, 'ident_f'], w=[pwk])
                    S.copy('act', tabW[:, 2 * gp:2 * gp + 2, d, 0, :], pw[:, 0:128].rearrange("p (t n) -> p t n", n=64), r=[pwk], w=['tabW'])
                    S.copy('act', tabW[:, 2 * gp:2 * gp + 2, d, 1, :], pw[:, 128:256].rearrange("p (t n) -> p t n", n=64), r=[pwk], w=['tabW'])
                S.mark('t4_%d' % d)
            S.dma('sp', tabK_d[:, :], tabK[:].rearrange("p g m -> p (g m)"), r=['tabK'], w=['tabK_d'])
            S.dma('sp', tabW_d[:, :], tabW[:].rearrange("p g d r n -> p (g d r n)"), r=['tabW'], w=['tabW_d'])
            S.dma('sp', tabM_d[:, :], tabM[:].rearrange("p g d r m -> p (g d r m)"), r=['tabM'], w=['tabM_d'])
            S.barrier()
            S.mark('tables')

        with ExitStack() as asx:
            adaw = T('adaw', [128, 8, 3 * D], F32, ctx=asx)
            cT = T('cT', [128, 8, NB], F32, ctx=asx)
            scT = T('scT', [128, 8, NB], F32, ctx=asx)
            adabT = T('adabT', [128, 24], F32, ctx=asx)
            normwT = T('normwT', [128, 8], F32, ctx=asx)
            adabg = T('adabg', [128, D], F32, ctx=asx)
            modT = T('modT', [128, 24, NB], F32, ctx=asx)
            for fc in range(8):
                S.dma('sp', adaw[:, fc, :], adaw_d[fc * 128:(fc + 1) * 128, :], w=[('adaw', fc)])
            S.dma('sp', cT[:], cT_d[:, :, :], w=['cT'])
            S.dma('sp', adabT[:], adabT_d[:, :], w=['adabT'])
            S.dma('sp', normwT[:], normwT_d[:, :], w=['normwT'])
            S.dma('sp', adabg[:], adabg_d[0:1, :].to_broadcast([128, D]), w=['adabg'])
            S.act(scT[:], cT[:], AF.Silu, r=['cT'], w=['scT'])
            ps.cfg([0, 1, 2, 3, 4, 5], [])
            pm, pmk = ps.full()
            for oc in range(24):
                for fc in range(8):
                    S.mm(pm[:, oc * NB:(oc + 1) * NB], lhsT=adaw[:, fc, oc * 128:(oc + 1) * 128], rhs=scT[:, fc, :],
                         start=(fc == 0), stop=(fc == 7), r=[('adaw', fc), 'scT'], w=[pmk])
            S.tt('dve', modT[:], pm[:, 0:24 * NB].rearrange("p (o b) -> p o b", b=NB),
                 adabT[:].unsqueeze(2).to_broadcast([128, 24, NB]), ALU.add, r=[pmk, 'adabT'], w=['modT'])
            S.copy('dve', shiftT[:], modT[:, 0:8, :], r=['modT'], w=['shiftT'])
            S.ts('dve', A_sc[:], modT[:, 8:16, :], 1.0, None, ALU.add, None, r=['modT'], w=['A_sc'])
            S.tt('dve', A_sc[:], A_sc[:], normwT[:].unsqueeze(2).to_broadcast([128, 8, NB]), ALU.mult,
                 r=['A_sc', 'normwT'], w=['A_sc'])
            for b in range(NB):
                for hf in range(2):
                    pg, pgk = ps.full()
                    for fc in range(8):
                        S.mm(pg[:, :], lhsT=scT[:, fc, b:b + 1].to_broadcast([128, 128]),
                             rhs=adaw[:, fc, 2 * D + hf * 512:2 * D + (hf + 1) * 512],
                             start=(fc == 0), stop=(fc == 7), r=[('adaw', fc), 'scT'], w=[pgk])
                    S.tt('dve', gate_row[:, b, hf * 512:(hf + 1) * 512], pg[:, :], adabg[:, hf * 512:(hf + 1) * 512], ALU.add,
                         r=[pgk, 'adabg'], w=[('gate_row', b)])
            S.barrier()
            S.mark('ada')

        hT = T('hT', [128, 8, L], BF16)

        eps_t = T('eps_t', [128, 1])
        lnq_t = T('lnq_t', [128, 1])
        S.memset('pool', eps_t[:], EPS, w=['eps_t'])
        S.memset('pool', lnq_t[:], -0.5 * math.log(128.0), w=['lnq_t'])

        for b in range(NB):
            ps.cfg([0, 1, 2, 3, 4, 5], [])
            with ExitStack() as px:
                sfx = '_p1_%d' % b
                xpool = Pool(S, px, 'xt' + sfx, [128, D], F32, 3)
                xspool = Pool(S, px, 'xs' + sfx, [128, D], BF16, 2)
                tmpool = Pool(S, px, 'tm' + sfx, [128, 8, 128], F32, 2)
                sspool = Pool(S, px, 'ss' + sfx, [128, 2], F32, 4)
                sq = T('sq' + sfx, [128, D], BF16, ctx=px)
                for t in range(NT):
                    xt, xk = xpool.next()
                    S.dma('sp', xt[:], x_d[b, t * 128:(t + 1) * 128, :], w=[xk])
                    ss, ssk = sspool.next()
                    S.memset('pool', ss[:], 0.0, w=[ssk])
                    S.act(sq[:], xt[:], AF.Square, accum=ss[:, 0:1], r=[xk], w=['sq', ssk])
                    S.act(ss[:, 1:2], ss[:, 0:1], AF.Ln, bias=eps_t[:, 0:1], scale=1.0 / D, r=[ssk, 'eps_t'], w=[ssk])
                    S.act(ss[:, 1:2], ss[:, 1:2], AF.Exp, scale=-0.5, r=[ssk], w=[ssk])
                    xs, xsk = xspool.next()
                    S.act(xs[:], xt[:], AF.Identity, scale=ss[:, 1:2], r=[xk, ssk], w=[xsk])
                    pb, pbk = ps.bfull()
                    for fc in range(8):
                        S.tr(pb[:, fc * 128:(fc + 1) * 128], xs[:, fc * 128:(fc + 1) * 128], ident_b[:], r=[xsk, 'ident_b'], w=[pbk])
                    tm, tmk = tmpool.next()
                    S.tt('dve', tm[:], pb[:].rearrange("p (f t) -> p f t", t=128),
                         A_sc[:, :, b].unsqueeze(2).to_broadcast([128, 8, 128]), ALU.mult, r=[pbk, 'A_sc'], w=[tmk])
                    S.tt('pool', hT[:, :, t * 128:(t + 1) * 128], tm[:],
                         shiftT[:, :, b].unsqueeze(2).to_broadcast([128, 8, 128]), ALU.add, r=[tmk, 'shiftT'], w=[('hT', t)])
                if dbg:
                    S.dma('sp', hTdbg_d[b], hT[:], r=[('hT', t) for t in range(NT)], w=['hTdbg'])
                S.barrier()
                S.mark('p1')

            ps.cfg([0, 1, 2, 3, 4, 5], [])
            with ExitStack() as sx:
                sfx = '_s5_%d' % b
                Wu = T('Wu' + sfx, [128, 8, 512], BF16, ctx=sx)
                Wza = T('Wza' + sfx, [128, 8, 512], BF16, ctx=sx)
                gluw = T('gluw' + sfx, [128, 4, 1024], BF16, ctx=sx)
                S.dma('pool', Wu[:], win_d[:, 0:512].rearrange("(fc p) c -> p fc c", p=128), w=['Wu'])
                S.dma('pool', Wza[:], win_d[:, 512:1024].rearrange("(fc p) c -> p fc c", p=128), w=['Wza'])
                S.dma('pool', gluw[:], gluw_d[:, :].rearrange("(kc p) o -> p kc o", p=128), w=['gluw'])
                UY = T('UY' + sfx, [128, 4, L], BF16, ctx=sx)
                UYu = UY[:].rearrange("p cc (g c) -> p cc g c", g=8)
                UTpool = Pool(S, sx, 'UT' + sfx, [128, 32, 8, 16], BF16, 1)
                tKp = Pool(S, sx, 'tK' + sfx, [128, 2, 128], BF16, 2)
                tWp = Pool(S, sx, 'tW' + sfx, [128, 2, 2, 2, 64], BF16, 2)
                tMp = Pool(S, sx, 'tM' + sfx, [128, 2, 2, 128], BF16, 2)
                PPt = [[[T('PP%d%d%d' % (d, ri, pp) + sfx, [128, NCK], F32, ctx=sx) for pp in range(2)] for ri in range(2)] for d in range(2)]
                Xbt = [[[T('Xb%d%d%d' % (d, ri, sl) + sfx, [128, NCK + 2], BF16, ctx=sx) for sl in range(2)] for ri in range(2)] for d in range(2)]
                for d in range(2):
                    for ri in range(2):
                        for sl in range(2):
                            S.memset('pool', Xbt[d][ri][sl][:], 0.0, w=[('Xb', d, ri, sl)])
                Ygp = Pool(S, sx, 'Yg' + sfx, [128, NCK], BF16, 2)
                YTp = Pool(S, sx, 'YT' + sfx, [128, NTB, 8, 128], BF16, 1)
                sgp = Pool(S, sx, 'sg' + sfx, [128, 512], F32, 2)
                szp = Pool(S, sx, 'sz' + sfx, [128, 512], F32, 2)
                yop = Pool(S, sx, 'yo' + sfx, [128, 512], BF16, 2)

                for tb in range(NTB):
                    UT, UTk = UTpool.next()
                    for j in range(8):
                        pp_, ppk = ps.full()
                        for fc in range(8):
                            S.mm(pp_[:, :], lhsT=hT[:, fc, tb * 1024 + j:(tb + 1) * 1024:8], rhs=Wu[:, fc, :],
                                 start=(fc == 0), stop=(fc == 7),
                                 r=[('hT', tb * 8 + q) for q in range(8)] + ['Wu'], w=[ppk])
                        S.copy('act' if j % 2 == 0 else 'dve', UT[:, :, j, :], pp_[:, :].rearrange("p (g q) -> p g q", q=16), r=[ppk], w=[UTk])
                    for g in range(32):
                        pq_, pqk = ps.bq()
                        S.tr(pq_, UT[:, g, :, :], ident_b[:], r=[UTk, 'ident_b'], w=[pqk])
                        S.copy('act' if g % 2 == 0 else 'dve', UYu[:, g // 8, g % 8, tb * 128:(tb + 1) * 128], pq_,
                               r=[pqk], w=[('U', g)] + [('yT', g // 8, q) for q in range(NTB)])

                YT = None
                for gp in range(16):
                    tK, tKk = tKp.next(); tW, tWk = tWp.next(); tM, tMk = tMp.next()
                    S.dma('sp', tK[:].rearrange("p t m -> p (t m)"), tabK_d[:, 2 * gp * 128:(2 * gp + 2) * 128], r=['tabK_d'], w=[tKk])
                    S.dma('sp', tW[:].rearrange("p t d r n -> p (t d r n)"), tabW_d[:, 2 * gp * 256:(2 * gp + 2) * 256], r=['tabW_d'], w=[tWk])
                    S.dma('sp', tM[:].rearrange("p d r m -> p (d r m)"), tabM_d[:, gp * 512:(gp + 1) * 512], r=['tabM_d'], w=[tMk])
                    sl = gp % 2
                    Ug = [UYu[:, (2 * gp + two) // 8, (2 * gp + two) % 8, :] for two in range(2)]
                    for d in range(2):
                        col = d * 16 + gp
                        for ri in range(2):
                            pS_, pSk = ps.full()
                            for two in range(2):
                                S.mm(pS_[64 * two:64 * two + 64, 0:NCK], lhsT=tW[:, two, d, ri, :], rhs=Ug[two],
                                     r=[tWk, ('U', 2 * gp + two)], w=[pSk])
                            S.copy('act', PPt[d][ri][0][:], pS_[:, 0:NCK], r=[pSk], w=[('PP', d, ri, 0)])
                        cur, oth = 0, 1
                        for k in range(NLV):
                            dd = 1 << k
                            last = (k == NLV - 1)
                            if d == 0:
                                dst = slice(dd, NCK); src = slice(0, NCK - dd); keep = slice(0, dd)
                                xdst = slice(1 + dd, 1 + NCK); xkeep = slice(1, 1 + dd)
                            else:
                                dst = slice(0, NCK - dd); src = slice(dd, NCK); keep = slice(NCK - dd, NCK)
                                xdst = slice(1, 1 + NCK - dd); xkeep = slice(1 + NCK - dd, 1 + NCK)
                            are = APre[:, col, k:k + 1]; aim = APim[:, col, k:k + 1]; naim = nAPim[:, col, k:k + 1]
                            cre, cim = PPt[d][0][cur], PPt[d][1][cur]
                            ore, oim = PPt[d][0][oth], PPt[d][1][oth]
                            kc_re, kc_im = ('PP', d, 0, cur), ('PP', d, 1, cur)
                            ko_re, ko_im = ('PP', d, 0, oth), ('PP', d, 1, oth)
                            S.stt(ore[:, dst], cim[:, src], naim, cre[:, dst], ALU.mult, ALU.add, r=[kc_re, kc_im, 'nAPim'], w=[ko_re])
                            S.stt(oim[:, dst], cre[:, src], aim, cim[:, dst], ALU.mult, ALU.add, r=[kc_re, kc_im, 'APim'], w=[ko_im])
                            if last:
                                S.stt(Xbt[d][0][sl][:, xdst], cre[:, src], are, ore[:, dst], ALU.mult, ALU.add,
                                      r=[kc_re, ko_re, 'APre'], w=[('Xb', d, 0, sl)])
                                S.stt(Xbt[d][1][sl][:, xdst], cim[:, src], are, oim[:, dst], ALU.mult, ALU.add,
                                      r=[kc_im, ko_im, 'APre'], w=[('Xb', d, 1, sl)])
                                S.copy('pool', Xbt[d][0][sl][:, xkeep], cre[:, keep], r=[kc_re], w=[('Xb', d, 0, sl)])
                                S.copy('pool', Xbt[d][1][sl][:, xkeep], cim[:, keep], r=[kc_im], w=[('Xb', d, 1, sl)])
                            else:
                                S.stt(ore[:, dst], cre[:, src], are, ore[:, dst], ALU.mult, ALU.add, r=[kc_re, ko_re, 'APre'], w=[ko_re])
                                S.stt(oim[:, dst], cim[:, src], are, oim[:, dst], ALU.mult, ALU.add, r=[kc_im, ko_im, 'APre'], w=[ko_im])
                                S.copy('pool', ore[:, keep], cre[:, keep], r=[kc_re], w=[ko_re])
                                S.copy('pool', oim[:, keep], cim[:, keep], r=[kc_im], w=[ko_im])
                            cur, oth = oth, cur
                    for two in range(2):
                        g = 2 * gp + two
                        cc = g // 8
                        if g % 8 == 0:
                            YT, YTk = YTp.next()
                        pY, pYk = ps.full()
                        S.mm(pY[:, 0:NCK], lhsT=tK[:, two, :], rhs=Ug[two], start=True, stop=False, r=[tKk, ('U', g)], w=[pYk])
                        for d in range(2):
                            off = 0 if d == 0 else 2
                            for ri in range(2):
                                S.mm(pY[:, 0:NCK], lhsT=tM[64 * two:64 * two + 64, d, ri, :],
                                     rhs=Xbt[d][ri][sl][64 * two:64 * two + 64, off:off + NCK],
                                     start=False, stop=(d == 1 and ri == 1), r=[tMk, ('Xb', d, ri, sl)], w=[pYk])
                        Yg, Ygk = Ygp.next()
                        S.act(Yg[:], pY[:, 0:NCK], AF.Gelu, r=[pYk], w=[Ygk])
                        pq_, pqk = ps.bfull()
                        for cb in range(NTB):
                            S.tr(pq_[:, cb * 128:(cb + 1) * 128], Yg[:, cb * 128:(cb + 1) * 128], ident_b[:], r=[Ygk, 'ident_b'], w=[pqk])
                        S.copy('dve', YT[:, :, :, (g % 8) * 16:(g % 8) * 16 + 16],
                               pq_[:, 0:NTB * 128].rearrange("p (c j q) -> p c j q", j=8, q=16), r=[pqk], w=[YTk])
                        if g % 8 == 7:
                            for tb in range(NTB):
                                pq_, pqk = ps.bfull()
                                for j in range(8):
                                    S.tr(pq_[:, j * 128:(j + 1) * 128], YT[:, tb, j, :], ident_b[:], r=[YTk, 'ident_b'], w=[pqk])
                                S.copy('act', UY[:, cc, tb * 1024:(tb + 1) * 1024].rearrange("p (c j) -> p c j", j=8),
                                       pq_[:].rearrange("p (j c) -> p c j", c=128), r=[pqk],
                                       w=[('yT', cc, tb)] + [('U', 8 * cc + q) for q in range(8)])

                for blk in range(NB5):
                    tsl = slice(blk * 512, (blk + 1) * 512)
                    for oc in range(4):
                        pa, pak = ps.full()
                        for kc in range(4):
                            S.mm(pa[:, :], lhsT=gluw[:, kc, oc * 128:(oc + 1) * 128], rhs=UY[:, kc, tsl], start=(kc == 0), stop=(kc == 3),
                                 r=['gluw', ('yT', kc, blk // 2)], w=[pak])
                        pg, pgk = ps.full()
                        for kc in range(4):
                            S.mm(pg[:, :], lhsT=gluw[:, kc, 512 + oc * 128:512 + (oc + 1) * 128], rhs=UY[:, kc, tsl], start=(kc == 0), stop=(kc == 3),
                                 r=['gluw', ('yT', kc, blk // 2)], w=[pgk])
                        pz, pzk = ps.full()
                        for fc in range(8):
                            S.mm(pz[:, :], lhsT=Wza[:, fc, oc * 128:(oc + 1) * 128], rhs=hT[:, fc, tsl], start=(fc == 0), stop=(fc == 7),
                                 r=['Wza'] + [('hT', blk * 4 + q) for q in range(4)], w=[pzk])
                        sg, sgk = sgp.next(); sz, szk = szp.next(); yo, yok = yop.next()
                        S.act(sg[:], pg[:, :], AF.Sigmoid, bias=glubT[:, 4 + oc:5 + oc], r=[pgk, 'glubT'], w=[sgk])
                        S.act(sz[:], pz[:, :], AF.Silu, r=[pzk], w=[szk])
                        S.stt(sg[:], pa[:, :], glubT[:, oc:oc + 1], sg[:], ALU.add, ALU.mult, r=[pak, 'glubT', sgk], w=[sgk])
                        S.tt('pool', yo[:], sg[:], sz[:], ALU.mult, r=[sgk, szk], w=[yok])
                        S.dma('sp', ycat_d[b, oc * 128:(oc + 1) * 128, tsl], yo[:], r=[yok], w=[('ycat', blk)], key=('ycat', blk))
                S.barrier()
                S.mark('s5')

            with ExitStack() as dx:
                sfx = '_dn_%d' % b
                pre = T('pre' + sfx, [128, 3, L + 4], BF16, ctx=dx)
                post = T('post' + sfx, [128, 3, L], BF16, ctx=dx)
                Oacc = T('Oacc' + sfx, [128, NT, 128], F32, ctx=dx)
                szb = T('szb' + sfx, [128, L], BF16, ctx=dx)
                Wqp = Pool(S, dx, 'Wq' + sfx, [128, 8, 3, 128], BF16, 1)
                Wzp = Pool(S, dx, 'Wz' + sfx, [128, 8, 128], BF16, 1)
                Wba = T('Wba' + sfx, [128, 8, 16], BF16, ctx=dx)
                diagw = T('diagw' + sfx, [128, 15, 128], BF16, ctx=dx)
                ba = T('ba' + sfx, [128, NT, 16], F32, ctx=dx)
                bsb = T('bsb' + sfx, [128, NT, 8], F32, ctx=dx)
                negb = T('negb' + sfx, [128, NT, 8], F32, ctx=dx)
                gsb = T('gsb' + sfx, [128, NT, 8], F32, ctx=dx)
                g1 = T('g1' + sfx, [128, NT, 8], F32, ctx=dx)
                g2 = T('g2' + sfx, [128, NT, 8], F32, ctx=dx)
                ssq = T('ssq' + sfx, [128, NT], F32, ctx=dx)
                rsq = T('rsq' + sfx, [128, NT], F32, ctx=dx)
                sqb = Pool(S, dx, 'sqb' + sfx, [128, 512], BF16, 2)
                lnp = Pool(S, dx, 'lnp' + sfx, [128, 512], F32, 2)
                f32p = Pool(S, dx, 'f32p' + sfx, [128, 128], F32, 12)
                b16p = Pool(S, dx, 'b16p' + sfx, [128, 128], BF16, 40)
                scp = Pool(S, dx, 'scp' + sfx, [128, 8], F32, 4)
                Sf = [T('Sf%d' % d + sfx, [128, 128], F32, ctx=dx) for d in range(2)]
                Sbf = [T('Sbf%d' % d + sfx, [128, 128], BF16, ctx=dx) for d in range(2)]
                ybp = Pool(S, dx, 'yb' + sfx, [128, 512], BF16, 2)
                junk2 = T('junk2' + sfx, [128, 128], BF16, ctx=dx)
                for cc3 in range(3):
                    S.memset('pool', pre[:, cc3, 0:2], 0.0, w=[('pre', cc3, 0)])
                    S.memset('pool', pre[:, cc3, L + 2:L + 4], 0.0, w=[('pre', cc3, NB5 - 1)])

                ps.cfg([0, 1], [2, 3, 4, 5])
                S.dma('pool', Wba[:], win_d[:, 3072:3088].rearrange("(fc p) c -> p fc c", p=128), w=['Wba'])
                pba, pbak = ps.full()
                for t in range(NT):
                    for fc in range(8):
                        S.mm(pba[:, t * 16:(t + 1) * 16], lhsT=hT[:, fc, t * 128:(t + 1) * 128], rhs=Wba[:, fc, :],
                             start=(fc == 0), stop=(fc == 7), r=[('hT', t), 'Wba'], w=[pbak])
                S.copy('dve', ba[:], pba[:, 0:NT * 16].rearrange("p (t c) -> p t c", c=16), r=[pbak], w=['ba'])
                S.act(bsb[:], ba[:, :, 0:8], AF.Sigmoid, r=['ba'], w=['bsb'])
                S.ts('dve', negb[:], bsb[:], -1.0, r=['bsb'], w=['negb'])
                S.tt('dve', g1[:], ba[:, :, 8:16], dtb_b[:].unsqueeze(1).to_broadcast([128, NT, 8]), ALU.add, r=['ba', 'dtb_b'], w=['g1'])
                S.stt(g2[:], g1[:], -1.0, g1[:], ALU.mult, ALU.max, r=['g1'], w=['g2'])
                S.act(g2[:], g2[:], AF.Exp, scale=-1.0, r=['g2'], w=['g2'])
                S.act(g2[:], g2[:], AF.Ln, bias=1.0, r=['g2'], w=['g2'])
                S.ts('dve', g1[:], g1[:], 0.0, None, ALU.max, None, r=['g1'], w=['g1'])
                S.tt('dve', g1[:], g1[:], g2[:], ALU.add, r=['g1', 'g2'], w=['g1'])
                S.tt('dve', gsb[:], g1[:], nexpa[:].unsqueeze(1).to_broadcast([128, NT, 8]), ALU.mult, r=['g1', 'nexpa'], w=['gsb'])
                S.mark('d0')

                for h in range(4):
                    Wq, Wqk = Wqp.next(); Wz, Wzk = Wzp.next()
                    for cc3 in range(3):
                        c0 = 1024 + cc3 * 512 + h * 128
                        S.dma('pool', Wq[:, :, cc3, :], win_d[:, c0:c0 + 128].rearrange("(fc p) c -> p fc c", p=128), w=[Wqk])
                    S.dma('pool', Wz[:], win_d[:, 2560 + h * 128:2560 + (h + 1) * 128].rearrange("(fc p) c -> p fc c", p=128), w=[Wzk])
                    for cc3 in range(3):
                        for j in range(5):
                            S.ts('pool', diagw[:, cc3 * 5 + j, :], ident_f[:], convw[:, cc3 * 4 + h, j:j + 1], r=['ident_f', 'convw'], w=['diagw'])
                    for cc3 in range(3):
                        for blk in range(NB5):
                            pp_, ppk = ps.full()
                            for fc in range(8):
                                S.mm(pp_[:, :], lhsT=Wq[:, fc, cc3, :], rhs=hT[:, fc, blk * 512:(blk + 1) * 512], start=(fc == 0), stop=(fc == 7),
                                     r=[Wqk] + [('hT', blk * 4 + q) for q in range(4)], w=[ppk])
                            S.copy('act' if blk % 2 == 0 else 'dve', pre[:, cc3, 2 + blk * 512:2 + (blk + 1) * 512], pp_[:, :], r=[ppk], w=[('pre', cc3, blk)])
                    for blk in range(NB5):
                        pp_, ppk = ps.full()
                        for fc in range(8):
                            S.mm(pp_[:, :], lhsT=Wz[:, fc, :], rhs=hT[:, fc, blk * 512:(blk + 1) * 512], start=(fc == 0), stop=(fc == 7),
                                 r=[Wzk] + [('hT', blk * 4 + q) for q in range(4)], w=[ppk])
                        S.act(szb[:, blk * 512:(blk + 1) * 512], pp_[:, :], AF.Silu, r=[ppk], w=[('szb', blk)])
                    for cc3 in range(3):
                        for blk in range(NB5):
                            pp_, ppk = ps.full()
                            for j in range(5):
                                S.mm(pp_[:, :], lhsT=diagw[:, cc3 * 5 + j, :], rhs=pre[:, cc3, blk * 512 + j:blk * 512 + j + 512], start=(j == 0), stop=(j == 4),
                                     r=['diagw'] + [('pre', cc3, q) for q in range(max(0, blk - 1), min(NB5, blk + 2))], w=[ppk])
                            S.act(post[:, cc3, blk * 512:(blk + 1) * 512], pp_[:, :], AF.Silu, r=[ppk], w=[('post', cc3, blk)])
                    for cc3 in range(2):
                        for blk in range(NB5):
                            sqt, sqk = sqb.next(); lnt, lnk = lnp.next()
                            pslc = post[:, cc3, blk * 512:(blk + 1) * 512]
                            S.act(sqt[:], pslc, AF.Square, r=[('post', cc3, blk)], w=[sqk])
                            pp_, ppk = ps.full()
                            S.mm(pp_[:, :], lhsT=ones_b[:], rhs=sqt[:], r=['ones_b', sqk], w=[ppk])
                            S.act(lnt[:], pp_[:, :], AF.Ln, bias=eps_t[:, 0:1], r=[ppk, 'eps_t'], w=[lnk])
                            if cc3 == 0:
                                S.act(lnt[:], lnt[:], AF.Exp, scale=-0.5, bias=lnq_t[:, 0:1], r=[lnk, 'lnq_t'], w=[lnk])
                            else:
                                S.act(lnt[:], lnt[:], AF.Exp, scale=-0.5, r=[lnk], w=[lnk])
                            S.tt('dve', pslc, pslc, lnt[:], ALU.mult, r=[('post', cc3, blk), lnk], w=[('post', cc3, blk)])
                    S.mark('d1')

                    for d in range(2):
                        S.memset('pool', Sf[d][:], 0.0, w=[('Sf', d)])
                        S.memset('pool', Sbf[d][:], 0.0, w=[('Sbf', d)])
                    visited = set()
                    for s_ in range(NT):
                        for d in range(2):
                            n = s_ if d == 0 else NT - 1 - s_
                            blk = n // 4
                            tok = slice(n * 128, (n + 1) * 128)
                            hd = d * 4 + h
                            qT_c = post[:, 0, tok]; kT_c = post[:, 1, tok]; vT_c = post[:, 2, tok]
                            kq, kk_, kvv = ('post', 0, blk), ('post', 1, blk), ('post', 2, blk)
                            gcol = gsb[:, n, hd:hd + 1]; bcol = bsb[:, n, hd:hd + 1]; nbcol = negb[:, n, hd:hd + 1]
                            tri, trik = (triF, 'triF') if d == 0 else (triB, 'triB')
                            lastc = 127 if d == 0 else 0
                            if d == 0:
                                mpat, mcm = [[1, 128]], -1
                            else:
                                mpat, mcm = [[-1, 128]], 1
                            pG, pGk = ps.quarter()
                            S.mm(pG, lhsT=gcol.to_broadcast([128, 128]), rhs=tri[:], r=['gsb', trik], w=[pGk])
                            pc, pck = ps.quarter()
                            S.mm(pc[:, 0:1], lhsT=tri[:], rhs=gcol, r=['gsb', trik], w=[pck])
                            sc, sck = scp.next()
                            S.act(sc[:, 0:1], pc[:, 0:1], AF.Identity, scale=-1.0, r=[pck], w=[sck])
                            S.act(sc[:, 1:2], pG[:, lastc:lastc + 1], AF.Identity, r=[pGk], w=[sck])
                            S.act(sc[:, 2:3], pc[:, 0:1], AF.Exp, r=[pck], w=[sck])
                            S.act(sc[:, 3:4], pc[:, 0:1], AF.Exp, scale=-1.0, bias=sc[:, 1:2], r=[pck, sck], w=[sck])
                            S.act(sc[:, 4:5], pG[:, lastc:lastc + 1], AF.Exp, r=[pGk], w=[sck])
                            S.mark('d2')
                            D1, D1k = f32p.next()
                            S.ts('dve', D1[:], pG, sc[:, 0:1], 0.0, ALU.add, ALU.min, r=[pGk, sck], w=[D1k])
                            D1m, D1mk = f32p.next()
                            S.act(D1m[:], D1[:], AF.Exp, r=[D1k], w=[D1mk])
                            GTi, GTik = f32p.next()
                            S.asel(GTi[:], D1m[:], mpat, ALU.is_ge, 0.0, 0, mcm, r=[D1mk], w=[GTik])
                            GTs, GTsk = f32p.next()
                            S.asel(GTs[:], GTi[:], mpat, ALU.is_gt, 0.0, 0, mcm, r=[GTik], w=[GTsk])
                            EG, EGk = f32p.next()
                            S.act(EG[:], pG, AF.Exp, r=[pGk], w=[EGk])
                            pKK, pKKk = ps.quarter()
                            S.mm(pKK, lhsT=kT_c, rhs=kT_c, r=[kk_], w=[pKKk])
                            pKQ, pKQk = ps.quarter()
                            S.mm(pKQ, lhsT=kT_c, rhs=qT_c, r=[kk_, kq], w=[pKQk])
                            U0, U0k = b16p.next()
                            S.stt(U0[:], pKK, bcol, GTs[:], ALU.mult, ALU.mult, r=[pKKk, 'bsb', GTsk], w=[U0k])
                            AT, ATk = b16p.next()
                            S.tt('dve', AT[:], pKQ, GTi[:], ALU.mult, r=[pKQk, GTik], w=[ATk])
                            qd, qdk = b16p.next()
                            S.tt('pool', qd[:], qT_c, EG[:], ALU.mult, r=[kq, EGk], w=[qdk])
                            pT_, pTk = ps.bq()
                            S.tr(pT_, kT_c, ident_b[:], r=[kk_, 'ident_b'], w=[pTk])
                            ktl, ktlk = b16p.next()
                            S.act(ktl[:], pT_, AF.Identity, scale=sc[:, 3:4], r=[pTk, sck], w=[ktlk])
                            pV_, pVk = ps.bq()
                            S.tr(pV_, vT_c, ident_b[:], r=[kvv, 'ident_b'], w=[pVk])
                            Vt, Vtk = b16p.next()
                            S.copy('act', Vt[:], pV_, r=[pVk], w=[Vtk])
                            pU_, pUk = ps.bq()
                            S.tr(pU_, U0[:], ident_b[:], r=[U0k, 'ident_b'], w=[pUk])
                            U0T, U0Tk = b16p.next()
                            S.copy('dve', U0T[:], pU_, r=[pUk], w=[U0Tk])
                            S.mark('d3')
                            tmpm, tmpmk = b16p.next()
                            S.tt('pool', tmpm[:], U0[:], lmask[:, 0, :], ALU.mult, r=[U0k, 'lmask'], w=[tmpmk])
                            Yt, Yk = b16p.next()
                            S.tt('pool', Yt[:], ident_b[:], tmpm[:], ALU.subtract, r=['ident_b', tmpmk], w=[Yk])
                            tmpm2, tmpm2k = b16p.next()
                            S.tt('pool', tmpm2[:], U0T[:], lmask[:, 0, :], ALU.mult, r=[U0Tk, 'lmask'], w=[tmpm2k])
                            YT_, YTk_ = b16p.next()
                            S.tt('pool', YT_[:], ident_b[:], tmpm2[:], ALU.subtract, r=['ident_b', tmpm2k], w=[YTk_])
                            for lv in range(1, 7):
                                p1, p1k = ps.quarter()
                                S.mm(p1, lhsT=U0T[:], rhs=Yt[:], r=[U0Tk, Yk], w=[p1k])
                                T1, T1k = b16p.next()
                                S.tt('dve', T1[:], p1, lmask[:, lv, :], ALU.mult, r=[p1k, 'lmask'], w=[T1k])
                                p2, p2k = ps.quarter()
                                S.mm(p2, lhsT=YT_[:], rhs=T1[:], r=[YTk_, T1k], w=[p2k])
                                Yn, Ynk = b16p.next()
                                S.tt('dve', Yn[:], Yt[:], p2, ALU.subtract, r=[Yk, p2k], w=[Ynk])
                                Yt, Yk = Yn, Ynk
                                if lv < 6:
                                    p3, p3k = ps.bq()
                                    S.tr(p3, Yt[:], ident_b[:], r=[Yk, 'ident_b'], w=[p3k])
                                    YTn, YTnk = b16p.next()
                                    S.copy('act', YTn[:], p3, r=[p3k], w=[YTnk])
                                    YT_, YTk_ = YTn, YTnk
                            NTt, NTk = Yt, Yk
                            pKS, pKSk = ps.quarter()
                            S.mm(pKS, lhsT=kT_c, rhs=Sbf[d][:], r=[kk_, ('Sbf', d)], w=[pKSk])
                            nR, nRk = b16p.next()
                            S.stt(nR[:], pKS, sc[:, 2:3], Vt[:], ALU.mult, ALU.subtract, r=[pKSk, sck, Vtk], w=[nRk])
                            pNR, pNRk = ps.quarter()
                            S.mm(pNR, lhsT=NTt[:], rhs=nR[:], r=[NTk, nRk], w=[pNRk])
                            vn, vnk = b16p.next()
                            S.act(vn[:], pNR, AF.Identity, scale=nbcol, r=[pNRk, 'negb'], w=[vnk])
                            S.mark('e2')
                            pO, pOk = ps.quarter()
                            S.mm(pO, lhsT=qd[:], rhs=Sbf[d][:], start=True, stop=False, r=[qdk, ('Sbf', d)], w=[pOk])
                            S.mm(pO, lhsT=AT[:], rhs=vn[:], start=False, stop=True, r=[ATk, vnk], w=[pOk])
                            pSn, pSnk = ps.quarter()
                            S.mm(pSn, lhsT=ktl[:], rhs=vn[:], r=[ktlk, vnk], w=[pSnk])
                            S.stt(Sf[d][:], Sf[d][:], sc[:, 4:5], pSn, ALU.mult, ALU.add, r=[('Sf', d), sck, pSnk], w=[('Sf', d)])
                            S.copy('act', Sbf[d][:], Sf[d][:], r=[('Sf', d)], w=[('Sbf', d)])
                            S.mark('e3')
                            if n not in visited:
                                visited.add(n)
                                S.copy('act', Oacc[:, n, :], pO, r=[pOk], w=[('Oacc', n)])
                            else:
                                S.tt('dve', Oacc[:, n, :], pO, Oacc[:, n, :], ALU.add, r=[pOk, ('Oacc', n)], w=[('Oacc', n)])
                            S.mark('d4')
                            S.mark('c_%d_%d_%d' % (h, s_, d))

                    S.mark('h%d' % h)
                    S.memset('pool', ssq[:], 0.0, w=['ssq'])
                    for n in range(NT):
                        S.act(junk2[:], Oacc[:, n, :], AF.Square, accum=ssq[:, n:n + 1], r=[('Oacc', n)], w=['junk2', 'ssq'])
                    S.act(rsq[:], ssq[:], AF.Ln, bias=eps_t[:, 0:1], scale=1.0 / 128.0, r=['ssq', 'eps_t'], w=['rsq'])
                    S.act(rsq[:], rsq[:], AF.Exp, scale=-0.5, r=['rsq'], w=['rsq'])
                    yb = None
                    for n in range(NT):
                        on, onk = b16p.next()
                        S.act(on[:], Oacc[:, n, :], AF.Identity, scale=rsq[:, n:n + 1], r=[('Oacc', n), 'rsq'], w=[onk])
                        pT_, pTk = ps.bq()
                        S.tr(pT_, on[:], ident_b[:], r=[onk, 'ident_b'], w=[pTk])
                        if n % 4 == 0:
                            yb, ybk = ybp.next()
                        S.stt(yb[:, (n % 4) * 128:(n % 4 + 1) * 128], pT_, dnw[:, 0:1], szb[:, n * 128:(n + 1) * 128], ALU.mult, ALU.mult,
                              r=[pTk, 'dnw', ('szb', n // 4)], w=[ybk])
                        if n % 4 == 3:
                            blk = n // 4
                            S.dma('sp', ycat_d[b, 512 + h * 128:512 + (h + 1) * 128, blk * 512:(blk + 1) * 512], yb[:],
                                  r=[ybk], w=[('ycat', blk)], key=('ycat', blk))
                S.barrier()
                S.mark('dn')

            ps.cfg([0, 1, 2, 3, 4, 5], [])
            with ExitStack() as fx:
                sfx = '_fin_%d' % b
                woutg = T('woutg' + sfx, [128, 8, D], BF16, ctx=fx)
                wstp = Pool(S, fx, 'wst' + sfx, [128, D], F32, 2)
                ycp = Pool(S, fx, 'ycp' + sfx, [128, 8, 512], BF16, 2)
                xpool = Pool(S, fx, 'xf' + sfx, [128, D], F32, 3)
                rp = Pool(S, fx, 'rp' + sfx, [128, D], F32, 2)
                op_ = Pool(S, fx, 'osb' + sfx, [128, D], F32, 2)
                sspool = Pool(S, fx, 'ssf' + sfx, [128, 2], F32, 4)
                sq = T('sqf' + sfx, [128, D], BF16, ctx=fx)
                for kc in range(8):
                    wst, wstk = wstp.next()
                    S.dma('sp', wst[:], wout_d[kc * 128:(kc + 1) * 128, :], w=[wstk])
                    S.tt('pool', woutg[:, kc, :], wst[:], gate_row[:, b, :], ALU.mult, r=[wstk, ('gate_row', b)], w=[('woutg', kc)])
                S.mark('f0')
                ycv = ycat_d[b].rearrange("(kc p) t -> p kc t", p=128)
                for blk in range(NB5):
                    yc, yck = ycp.next()
                    S.dma('sp', yc[:], ycv[:, :, blk * 512:(blk + 1) * 512], r=[('ycat', blk)], w=[yck])
                    for t4 in range(4):
                        t = blk * 4 + t4
                        xt, xk = xpool.next()
                        S.dma('sp', xt[:], x_d[b, t * 128:(t + 1) * 128, :], w=[xk])
                        rr, rk = rp.next()
                        for hf in range(2):
                            po, pok = ps.full()
                            for kc in range(8):
                                S.mm(po[:, :], lhsT=yc[:, kc, t4 * 128:(t4 + 1) * 128], rhs=woutg[:, kc, hf * 512:(hf + 1) * 512],
                                     start=(kc == 0), stop=(kc == 7), r=[yck, ('woutg', kc)], w=[pok])
                            S.tt('dve', rr[:, hf * 512:(hf + 1) * 512], po[:, :], xt[:, hf * 512:(hf + 1) * 512], ALU.add, r=[pok, xk], w=[rk])
                        S.mark('f1')
                        ss, ssk = sspool.next()
                        S.memset('pool', ss[:], 0.0, w=[ssk])
                        S.act(sq[:], rr[:], AF.Square, accum=ss[:, 0:1], r=[rk], w=['sqf', ssk])
                        S.act(ss[:, 1:2], ss[:, 0:1], AF.Ln, bias=eps_t[:, 0:1], scale=1.0 / D, r=[ssk, 'eps_t'], w=[ssk])
                        S.act(ss[:, 1:2], ss[:, 1:2], AF.Exp, scale=-0.5, r=[ssk], w=[ssk])
                        ot, otk = op_.next()
                        S.stt(ot[:], rr[:], ss[:, 1:2], fnw_row[:], ALU.mult, ALU.mult, r=[rk, ssk, 'fnw_row'], w=[otk])
                        S.dma('sp', out_d[b, t * 128:(t + 1) * 128, :], ot[:], r=[otk], w=[('out', t % 2)], key=('out', t % 2))
                        S.mark('f2')
                S.barrier()

        S.barrier()
    return nc


def _prep_shared(inp):
    f32 = np.float32
    g = lambda k: np.asarray(inp[k], dtype=f32)[0]
    sh = {}
    sh["ada_w"] = np.ascontiguousarray(g("ada_w"))
    ada_b = g("ada_b")
    sh["ada_bT"] = np.ascontiguousarray(ada_b.reshape(24, 128).T)
    sh["ada_bg"] = np.ascontiguousarray(ada_b[2 * D:3 * D].reshape(1, D))
    sh["norm_wT"] = np.ascontiguousarray(g("norm_w").reshape(8, 128).T)
    sh["w_in"] = np.ascontiguousarray(g("w_in"))
    sh["conv_wT"] = np.ascontiguousarray(g("conv_w").reshape(5, 12, 128).transpose(2, 1, 0))
    sh["dn_alog"] = np.concatenate([g("dn_a_log_f"), g("dn_a_log_b")]).reshape(1, 8).astype(f32)
    sh["dn_dtb"] = np.concatenate([g("dn_dt_bias_f"), g("dn_dt_bias_b")]).reshape(1, 8).astype(f32)
    sh["dn_norm_w"] = np.ascontiguousarray(g("dn_norm_w").reshape(128, 1))

    def n2(a):
        a = a.reshape((16, 2, 64) + a.shape[2:])
        a = np.moveaxis(a, 0, 2)
        return np.ascontiguousarray(a.reshape((128, 16) + a.shape[3:]))
    lamre, lamim, lstep, Bre, Bim, CTre, CTim = [], [], [], [], [], [], []
    for d in ("f", "b"):
        lamre.append(n2(g("lam_re_" + d)))
        lamim.append(n2(g("lam_im_" + d)))
        ls = np.repeat(g("log_step_" + d)[:, None], 64, axis=1)
        lstep.append(n2(ls))
        Bre.append(n2(g("b_re_" + d)))
        Bim.append(n2(g("b_im_" + d)))
        CTre.append(n2(g("c_re_" + d).transpose(0, 2, 1)))
        CTim.append(n2(g("c_im_" + d).transpose(0, 2, 1)))
    cat = lambda l: np.ascontiguousarray(np.concatenate(l, axis=1).astype(f32))
    sh["lamre"], sh["lamim"], sh["lstep"] = cat(lamre), cat(lamim), cat(lstep)
    sh["Bre"], sh["Bim"], sh["CTre"], sh["CTim"] = cat(Bre), cat(Bim), cat(CTre), cat(CTim)
    s5d = g("s5_d").reshape(32, 16)
    sh["Dcol"] = np.ascontiguousarray(np.tile(s5d.T[None, :, :], (8, 1, 1)).reshape(128, 32))
    sh["glu_w"] = np.ascontiguousarray(g("glu_w"))
    sh["glu_bT"] = np.ascontiguousarray(g("glu_b").reshape(8, 128).T)
    sh["w_out"] = np.ascontiguousarray(g("w_out"))
    sh["fnw"] = np.ascontiguousarray(np.asarray(inp["final_norm_w"], dtype=f32).reshape(1, D))
    return sh


def _core_map(sh, xs, cs):
    m = dict(sh)
    m["x"] = np.ascontiguousarray(xs, dtype=np.float32)
    nb = cs.shape[0]
    m["cT"] = np.ascontiguousarray(np.asarray(cs, dtype=np.float32).reshape(nb, 8, 128).transpose(2, 1, 0))
    return m


def kernel(**inputs):
    x = np.asarray(inputs["x"], dtype=np.float32)
    c = np.asarray(inputs["c"], dtype=np.float32)
    B, L, _ = x.shape
    NB = B // NCORES
    sh = _prep_shared(inputs)
    nc = build_nc(L, NB)
    in_maps = [_core_map(sh, x[i * NB:(i + 1) * NB], c[i * NB:(i + 1) * NB]) for i in range(NCORES)]
    res = run_bass_kernel_spmd(nc, in_maps, core_ids=list(range(NCORES)))
    return np.concatenate([np.asarray(r["out"]) for r in res.results], axis=0).astype(np.float32)
```

```python
import math
import numpy as np
from contextlib import ExitStack
import concourse.bass as bass
import concourse.mybir as mybir
from concourse.bass_utils import run_bass_kernel_spmd

F32 = mybir.dt.float32
BF16 = mybir.dt.bfloat16
AF = mybir.ActivationFunctionType
ALU = mybir.AluOpType

D = 1024
NCORES = 8
EPS = 1e-6
MAGIC = 12582912.0
TWO_PI_S = 6.28318
NEGBIG = -30000.0
IN_COLS = 3088


class Sched:
    def __init__(self, nc, es):
        self.nc = nc
        self.es = es
        self.engs = {'pe': nc.tensor, 'act': nc.scalar, 'dve': nc.vector, 'pool': nc.gpsimd, 'sp': nc.sync}
        self.sem = {k: es.enter_context(nc.semaphore('s_' + k)) for k in ['pe', 'act', 'dve', 'pool']}
        self.cnt = {k: 0 for k in self.sem}
        self.waited = {k: {} for k in self.engs}
        self.lw = {}
        self.rd = {}
        self.dsem = {}
        self.nwaits = 0
        self.dead = False
        self.stop_at = None

    def _handle(self, k):
        return self.sem[k] if k in self.sem else self.dsem[k][0]

    def _waits(self, e, reads, writes):
        hard = {}
        soft = {}

        def add(d, ev):
            if ev is not None and d.get(ev[0], 0) < ev[1]:
                d[ev[0]] = ev[1]
        for r in reads:
            add(hard, self.lw.get(r))
        for w in writes:
            add(hard, self.lw.get(w))
            for k, v in self.rd.get(w, {}).items():
                add(soft, (k, v))
        for k, v in soft.items():
            add(hard, (k, v))
        eng = self.engs[e]
        for k, v in hard.items():
            if k == e and e == 'pe':
                continue
            if self.waited[e].get(k, 0) < v:
                eng.wait_ge(self._handle(k), v)
                self.waited[e][k] = v
                self.nwaits += 1

    def mark(self, name):
        if self.stop_at == name and not self.dead:
            self.barrier()
            self.dead = True

    def op(self, e, fn, r=(), w=()):
        if self.dead:
            return
        self._waits(e, r, w)
        ins = fn(self.engs[e])
        self.cnt[e] += 1
        ins.then_inc(self.sem[e], 1)
        ev = (e, self.cnt[e])
        for x in w:
            self.lw[x] = ev
            self.rd[x] = {}
        for x in r:
            self.rd.setdefault(x, {})[e] = self.cnt[e]

    def dma(self, q, out, in_, r=(), w=(), key=None, **kw):
        if self.dead:
            return
        self._waits(q, r, w)
        if key is None:
            key = w[0] if w else r[0]
        sk = ('d', key)
        if sk not in self.dsem:
            self.dsem[sk] = [self.es.enter_context(self.nc.semaphore('d%d' % len(self.dsem))), 0]
        self.dsem[sk][1] += 16
        self.engs[q].dma_start(out=out, in_=in_, **kw).then_inc(self.dsem[sk][0], 16)
        ev = (sk, self.dsem[sk][1])
        for x in w:
            self.lw[x] = ev
            self.rd[x] = {}
        for x in r:
            self.rd.setdefault(x, {})[sk] = self.dsem[sk][1]

    def barrier(self):
        if self.dead:
            return
        evs = {k: v for k, v in self.cnt.items() if v > 0}
        for sk, (h, c) in self.dsem.items():
            if c > 0:
                evs[sk] = c
        for e in self.engs:
            for k, v in evs.items():
                if self.waited[e].get(k, 0) < v:
                    self.engs[e].wait_ge(self._handle(k), v)
                    self.waited[e][k] = v

    def mm(self, out, lhsT, rhs, start=True, stop=True, r=(), w=()):
        self.op('pe', lambda e: e.matmul(out, lhsT=lhsT, rhs=rhs, start=start, stop=stop), r, w)

    def tr(self, out, in_, ident, r=(), w=()):
        self.op('pe', lambda e: e.transpose(out=out, in_=in_, identity=ident), r, w)

    def act(self, out, in_, func, bias=None, scale=None, accum=None, r=(), w=()):
        kw = {}
        if bias is not None:
            kw['bias'] = bias
        if scale is not None:
            kw['scale'] = scale
        if accum is not None:
            kw['accum_out'] = accum
        self.op('act', lambda e: e.activation(out=out, in_=in_, func=func, **kw), r, w)

    def tt(self, eng, out, in0, in1, op, r=(), w=()):
        self.op(eng, lambda e: e.tensor_tensor(out=out, in0=in0, in1=in1, op=op), r, w)

    def ts(self, eng, out, in0, s1, s2=None, op0=ALU.mult, op1=None, r=(), w=()):
        if op1 is None:
            self.op(eng, lambda e: e.tensor_scalar(out=out, in0=in0, scalar1=s1, scalar2=None, op0=op0), r, w)
        else:
            self.op(eng, lambda e: e.tensor_scalar(out=out, in0=in0, scalar1=s1, scalar2=s2, op0=op0, op1=op1), r, w)

    def stt(self, out, in0, scalar, in1, op0, op1, r=(), w=()):
        self.op('dve', lambda e: e.scalar_tensor_tensor(out=out, in0=in0, scalar=scalar, in1=in1, op0=op0, op1=op1), r, w)

    def copy(self, eng, out, in_, r=(), w=()):
        if eng == 'act':
            self.op('act', lambda e: e.copy(out=out, in_=in_), r, w)
        else:
            self.op(eng, lambda e: e.tensor_copy(out=out, in_=in_), r, w)

    def memset(self, eng, ap, val, w=()):
        self.op(eng, lambda e: e.memset(ap, val), (), w)

    def asel(self, out, in_, pattern, cmp, fill, base, cm, r=(), w=()):
        self.op('pool', lambda e: e.affine_select(out=out, in_=in_, pattern=pattern, compare_op=cmp, fill=fill,
                                                  base=base, channel_multiplier=cm), r, w)


class Pool:
    def __init__(self, S, ctx, name, shape, dt, n, space='sbuf'):
        self.S = S
        self.name = name
        self.n = n
        self.i = 0
        alloc = S.nc.sbuf_tensor if space == 'sbuf' else S.nc.psum_tensor
        self.t = [ctx.enter_context(alloc('sbp_%s_%d' % (name, k), shape, dt)) for k in range(n)]

    def next(self):
        k = self.i % self.n
        self.i += 1
        key = (self.name, k)
        if key in self.S.lw and not self.S.rd.get(key) and not self.S.dead:
            raise RuntimeError('pool slot %s reused before being read' % (key,))
        return self.t[k], key


def build_nc(L, NB, dbg=False, stop_at=None):
    assert L % 1024 == 0
    NT = L // 128
    NCK = L // 8
    NTB = L // 1024
    NB5 = L // 512
    NLV = int(round(math.log2(NCK)))
    nc = bass.Bass("TRN2", target_bir_lowering=False)

    def din(name, shape, dt=F32):
        return nc.dram_tensor(name, shape, dt, kind="ExternalInput").ap()
    x_d = din("x", [NB, L, D])
    cT_d = din("cT", [128, 8, NB])
    adaw_d = din("ada_w", [D, 3 * D])
    adabT_d = din("ada_bT", [128, 24])
    adabg_d = din("ada_bg", [1, D])
    normwT_d = din("norm_wT", [128, 8])
    win_d = din("w_in", [D, IN_COLS])
    convw_d = din("conv_wT", [128, 12, 5])
    alog_d = din("dn_alog", [1, 8])
    dtb_d = din("dn_dtb", [1, 8])
    dnw_d = din("dn_norm_w", [128, 1])
    lamre_d = din("lamre", [128, 32])
    lamim_d = din("lamim", [128, 32])
    lstep_d = din("lstep", [128, 32])
    Bre_d = din("Bre", [128, 32, 16])
    Bim_d = din("Bim", [128, 32, 16])
    CTre_d = din("CTre", [128, 32, 16])
    CTim_d = din("CTim", [128, 32, 16])
    Dcol_d = din("Dcol", [128, 32])
    gluw_d = din("glu_w", [512, 1024])
    glubT_d = din("glu_bT", [128, 8])
    wout_d = din("w_out", [D, D])
    fnw_d = din("fnw", [1, D])
    out_d = nc.dram_tensor("out", [NB, L, D], F32, kind="ExternalOutput").ap()
    skind = "ExternalOutput" if dbg else "Internal"
    ycat_d = nc.dram_tensor("ycat", [NB, D, L], BF16, kind=skind).ap()
    tabK_d = nc.dram_tensor("tabK", [128, 32 * 128], BF16, kind="Internal").ap()
    tabW_d = nc.dram_tensor("tabW", [128, 32 * 256], BF16, kind="Internal").ap()
    tabM_d = nc.dram_tensor("tabM", [128, 16 * 512], BF16, kind="Internal").ap()
    if dbg:
        hTdbg_d = nc.dram_tensor("hTdbg", [NB, 128, 8, L], BF16, kind="ExternalOutput").ap()

    with ExitStack() as es:
        S = Sched(nc, es)
        S.stop_at = stop_at

        def T(name, shape, dt=F32, ctx=es):
            return ctx.enter_context(nc.sbuf_tensor('sb_' + name, shape, dt))

        psF_t = [es.enter_context(nc.psum_tensor('psF%d' % i, [128, 512], F32)) for i in range(6)]
        psB_t = [es.enter_context(nc.psum_tensor('psB%d' % i, [128, 1024], BF16)) for i in range(2)]

        class PS:
            def __init__(self):
                self.iF = 0
                self.iQ = 0
                self.iB = 0
                self.iBq = 0
                self.fullbanks = [0, 1, 2, 3, 4, 5]
                self.qbanks = []

            def cfg(self, full, q):
                self.fullbanks = full
                self.qbanks = q
                self.iF = 0
                self.iQ = 0

            def full(self):
                b = self.fullbanks[self.iF % len(self.fullbanks)]
                self.iF += 1
                return psF_t[b], ('psF', b)

            def quarter(self):
                nb_ = len(self.qbanks)
                k = self.iQ % (nb_ * 4)
                self.iQ += 1
                b = self.qbanks[k % nb_]
                q = k // nb_
                return psF_t[b][:, q * 128:(q + 1) * 128], ('psF', b)

            def bfull(self):
                b = self.iB % 2
                self.iB += 1
                return psB_t[b], ('psB', b)

            def bq(self):
                k = self.iBq % 16
                self.iBq += 1
                return psB_t[k % 2][:, (k // 2) * 128:(k // 2 + 1) * 128], ('psB', k % 2)
        ps = PS()

        ones_f = T('ones_f', [128, 128])
        ident_f = T('ident_f', [128, 128])
        ident_b = T('ident_b', [128, 128], BF16)
        ones_b = T('ones_b', [128, 128], BF16)
        triF = T('triF', [128, 128])
        triB = T('triB', [128, 128])
        S.memset('pool', ones_f[:], 1.0, w=['ones_f'])
        S.asel(ident_f[:], ones_f[:], [[-1, 128]], ALU.is_equal, 0.0, 0, 1, r=['ones_f'], w=['ident_f'])
        S.copy('pool', ident_b[:], ident_f[:], r=['ident_f'], w=['ident_b'])
        S.copy('pool', ones_b[:], ones_f[:], r=['ones_f'], w=['ones_b'])
        S.asel(triF[:], ones_f[:], [[1, 128]], ALU.is_ge, 0.0, 0, -1, r=['ones_f'], w=['triF'])
        S.asel(triB[:], ones_f[:], [[-1, 128]], ALU.is_ge, 0.0, 0, 1, r=['ones_f'], w=['triB'])
        lmask = T('lmask', [128, 7, 128])
        bsx = ExitStack()
        Bs = T('Bs', [128, 8, 128], F32, ctx=bsx)
        for li in range(8):
            s_ = 1 << li
            nb_ = 128 // s_
            if s_ == 128:
                S.copy('pool', Bs[:, li, :], ones_f[:], r=['ones_f'], w=['Bs'])
            else:
                S.asel(Bs[:, li, :].rearrange("p (b r) -> p b r", r=s_), ones_f[:].rearrange("p (b r) -> p b r", r=s_),
                       [[-s_, nb_], [0, s_]], ALU.is_ge, 0.0, 0, 1, r=['ones_f'], w=['Bs'])
                S.asel(Bs[:, li, :].rearrange("p (b r) -> p b r", r=s_), Bs[:, li, :].rearrange("p (b r) -> p b r", r=s_),
                       [[s_, nb_], [0, s_]], ALU.is_ge, 0.0, s_ - 1, -1, r=['Bs'], w=['Bs'])
        for li in range(7):
            S.tt('pool', lmask[:, li, :], Bs[:, li + 1, :], Bs[:, li, :], ALU.subtract, r=['Bs'], w=['lmask'])
        S.barrier()
        bsx.close()
        APre = T('APre', [128, 32, 10])
        APim = T('APim', [128, 32, 10])
        nAPim = T('nAPim', [128, 32, 10])
        A_sc = T('A_sc', [128, 8, NB])
        shiftT = T('shiftT', [128, 8, NB])
        gate_row = T('gate_row', [128, NB, D])
        fnw_row = T('fnw_row', [128, D])
        glubT = T('glubT', [128, 8])
        convw = T('convw', [128, 12, 5])
        dnw = T('dnw', [128, 1])
        alog_b = T('alog_b', [128, 8])
        dtb_b = T('dtb_b', [128, 8])
        nexpa = T('nexpa', [128, 8])
        S.dma('sp', fnw_row[:], fnw_d[0:1, :].to_broadcast([128, D]), w=['fnw_row'])
        S.dma('sp', glubT[:], glubT_d[:, :], w=['glubT'])
        S.dma('sp', convw[:], convw_d[:, :, :], w=['convw'])
        S.dma('sp', dnw[:], dnw_d[:, :], w=['dnw'])
        S.dma('sp', alog_b[:], alog_d[0:1, :].to_broadcast([128, 8]), w=['alog_b'])
        S.dma('sp', dtb_b[:], dtb_d[0:1, :].to_broadcast([128, 8]), w=['dtb_b'])
        S.act(nexpa[:], alog_b[:], AF.Exp, r=['alog_b'], w=['nexpa'])
        S.ts('dve', nexpa[:], nexpa[:], -1.0, r=['nexpa'], w=['nexpa'])
        S.mark('t0')

        with ExitStack() as tsx:
            def TT(name, shape, dt=F32):
                return T(name, shape, dt, ctx=tsx)
            lamre = TT('lamre', [128, 32]); lamim = TT('lamim', [128, 32]); lstep = TT('lstep', [128, 32])
            Bre = TT('Bre', [128, 32, 16]); Bim = TT('Bim', [128, 32, 16])
            CTre = TT('CTre', [128, 32, 16]); CTim = TT('CTim', [128, 32, 16])
            Dcol = TT('Dcol', [128, 32])
            for nm, t, d_ in [('lamre', lamre, lamre_d), ('lamim', lamim, lamim_d), ('lstep', lstep, lstep_d),
                              ('Dcol', Dcol, Dcol_d)]:
                S.dma('sp', t[:], d_[:, :], w=[nm])
            for nm, t, d_ in [('Bre', Bre, Bre_d), ('Bim', Bim, Bim_d), ('CTre', CTre, CTre_d), ('CTim', CTim, CTim_d)]:
                S.dma('sp', t[:], d_[:, :, :], w=[nm])
            kv = TT('kv', [128, 1, 40])
            kvals = [-s for s in range(8)] + [7 - s for s in range(8)] + [s for s in range(8)] + \
                    [s + 1 for s in range(8)] + [8 - s for s in range(8)]
            for s_, k_ in enumerate(kvals):
                S.memset('pool', kv[:, :, s_:s_ + 1], float(k_), w=['kv'])
            SL_NEG, SL_REV7, SL_POS, SL_POS1, SL_REV8 = 0, 8, 16, 24, 32
            dtt = TT('dtt', [128, 32]); lrd = TT('lrd', [128, 32]); lid = TT('lid', [128, 32])
            S.act(dtt[:], lstep[:], AF.Exp, r=['lstep'], w=['dtt'])
            S.tt('dve', lrd[:], lamre[:], dtt[:], ALU.mult, r=['lamre', 'dtt'], w=['lrd'])
            S.tt('dve', lid[:], lamim[:], dtt[:], ALU.mult, r=['lamim', 'dtt'], w=['lid'])
            kvb = kv[:].to_broadcast([128, 32, 40])
            PWre = TT('PWre', [128, 32, 40]); PWim = TT('PWim', [128, 32, 40])
            pwx = ExitStack()
            w1 = T('w1', [128, 32, 40], F32, ctx=pwx); w2 = T('w2', [128, 32, 40], F32, ctx=pwx); w3 = T('w3', [128, 32, 40], F32, ctx=pwx)
            mag = T('mag', [128, 32, 40], F32, ctx=pwx)
            lrdb = lrd[:].unsqueeze(2).to_broadcast([128, 32, 40])
            lidb = lid[:].unsqueeze(2).to_broadcast([128, 32, 40])
            S.tt('dve', w1[:], lrdb, kvb, ALU.mult, r=['lrd', 'kv'], w=['w1'])
            S.act(mag[:], w1[:], AF.Exp, r=['w1'], w=['mag'])
            S.stt(w1[:], lidb, 1.0 / (2.0 * math.pi), kvb, ALU.mult, ALU.mult, r=['lid', 'kv', 'mag'], w=['w1'])
            S.ts('dve', w2[:], w1[:], MAGIC, MAGIC, ALU.add, ALU.subtract, r=['w1'], w=['w2'])
            S.tt('dve', w2[:], w1[:], w2[:], ALU.subtract, r=['w1', 'w2'], w=['w2'])
            S.act(w3[:], w2[:], AF.Sin, scale=TWO_PI_S, r=['w2'], w=['w3'])
            S.tt('dve', PWim[:], mag[:], w3[:], ALU.mult, r=['mag', 'w3'], w=['PWim'])
            S.ts('dve', w1[:], w1[:], 0.25, None, ALU.add, None, r=['w1'], w=['w1'])
            S.ts('dve', w2[:], w1[:], MAGIC, MAGIC, ALU.add, ALU.subtract, r=['w1'], w=['w2'])
            S.tt('dve', w2[:], w1[:], w2[:], ALU.subtract, r=['w1', 'w2'], w=['w2'])
            S.act(w3[:], w2[:], AF.Sin, scale=TWO_PI_S, r=['w2'], w=['w3'])
            S.tt('dve', PWre[:], mag[:], w3[:], ALU.mult, r=['mag', 'w3'], w=['PWre'])
            S.mark('t1')
            S.barrier()
            pwx.close()
            a1re = PWre[:, :, SL_POS + 1]; a1im = PWim[:, :, SL_POS + 1]
            nr = TT('nr', [128, 32]); den = TT('den', [128, 32]); q1 = TT('q1', [128, 32]); q2 = TT('q2', [128, 32])
            fre = TT('fre', [128, 32]); fim = TT('fim', [128, 32])
            S.ts('dve', nr[:], a1re, -1.0, None, ALU.add, None, r=['PWre'], w=['nr'])
            S.tt('dve', den[:], lamre[:], lamre[:], ALU.mult, r=['lamre'], w=['den'])
            S.tt('dve', q1[:], lamim[:], lamim[:], ALU.mult, r=['lamim'], w=['q1'])
            S.tt('dve', den[:], den[:], q1[:], ALU.add, r=['den', 'q1'], w=['den'])
            S.op('dve', lambda e: e.reciprocal(out=den[:], in_=den[:]), r=['den'], w=['den'])
            S.tt('dve', q1[:], nr[:], lamre[:], ALU.mult, r=['nr', 'lamre', 'den'], w=['q1'])
            S.tt('dve', q2[:], a1im, lamim[:], ALU.mult, r=['PWim', 'lamim'], w=['q2'])
            S.tt('dve', q1[:], q1[:], q2[:], ALU.add, r=['q1', 'q2'], w=['q1'])
            S.tt('dve', fre[:], q1[:], den[:], ALU.mult, r=['q1', 'den'], w=['fre'])
            S.tt('dve', q1[:], a1im, lamre[:], ALU.mult, r=['PWim', 'lamre', 'fre'], w=['q1'])
            S.tt('dve', q2[:], nr[:], lamim[:], ALU.mult, r=['nr', 'lamim', 'q1'], w=['q2'])
            S.tt('dve', q1[:], q1[:], q2[:], ALU.subtract, r=['q1', 'q2'], w=['q1'])
            S.tt('dve', fim[:], q1[:], den[:], ALU.mult, r=['q1', 'den'], w=['fim'])
            BBre = TT('BBre', [128, 32, 16]); BBim = TT('BBim', [128, 32, 16])
            u1 = TT('u1', [128, 32, 16]); u2 = TT('u2', [128, 32, 16])
            freb = fre[:].unsqueeze(2).to_broadcast([128, 32, 16]); fimb = fim[:].unsqueeze(2).to_broadcast([128, 32, 16])
            S.tt('dve', u1[:], Bre[:], freb, ALU.mult, r=['Bre', 'fre'], w=['u1'])
            S.tt('dve', u2[:], Bim[:], fimb, ALU.mult, r=['Bim', 'fim'], w=['u2'])
            S.tt('dve', BBre[:], u1[:], u2[:], ALU.subtract, r=['u1', 'u2'], w=['BBre'])
            S.tt('dve', u1[:], Bim[:], freb, ALU.mult, r=['Bim', 'fre', 'BBre'], w=['u1'])
            S.tt('dve', u2[:], Bre[:], fimb, ALU.mult, r=['Bre', 'fim', 'BBre'], w=['u2'])
            S.tt('dve', BBim[:], u1[:], u2[:], ALU.add, r=['u1', 'u2'], w=['BBim'])
            S.copy('dve', APre[:, :, 0], PWre[:, :, SL_POS1 + 7], r=['PWre'], w=['APre'])
            S.copy('dve', APim[:, :, 0], PWim[:, :, SL_POS1 + 7], r=['PWim'], w=['APim'])
            for k in range(9):
                S.tt('dve', q1[:], APre[:, :, k], APre[:, :, k], ALU.mult, r=['APre', 'q1'], w=['q1'])
                S.tt('dve', q2[:], APim[:, :, k], APim[:, :, k], ALU.mult, r=['APim', 'q2'], w=['q2'])
                S.tt('dve', APre[:, :, k + 1], q1[:], q2[:], ALU.subtract, r=['q1', 'q2'], w=['APre'])
                S.stt(APim[:, :, k + 1], APre[:, :, k], 2.0, APim[:, :, k], ALU.mult, ALU.mult, r=['APre', 'APim'], w=['APim'])
            S.ts('dve', nAPim[:], APim[:], -1.0, r=['APim'], w=['nAPim'])
            S.mark('t2')

            tabK = TT('tabK', [128, 32, 128], BF16)
            tabW = TT('tabW', [128, 32, 2, 2, 64], BF16)
            tabM = TT('tabM', [128, 16, 2, 2, 128], BF16)
            Kacc = TT('Kacc', [128, 32, 128])
            mKf = TT('mKf', [128, 8, 16]); mKb = TT('mKb', [128, 8, 16])
            S.asel(mKf[:], ones_f[:].rearrange("p (j q) -> p j q", q=16), [[16, 8], [0, 16]], ALU.is_ge, 0.0, 15, -1,
                   r=['ones_f'], w=['mKf'])
            S.asel(mKb[:], ones_f[:].rearrange("p (j q) -> p j q", q=16), [[-16, 8], [0, 16]], ALU.is_ge, 0.0, 0, 1,
                   r=['ones_f'], w=['mKb'])
            S.mark('t3')
            Gre = TT('Gre', [128, 16, 8, 16]); Gim = TT('Gim', [128, 16, 8, 16])
            G7re = TT('G7re', [128, 16, 8, 16]); G7im = TT('G7im', [128, 16, 8, 16])
            Hre = TT('Hre', [128, 16, 8, 16]); Hnim = TT('Hnim', [128, 16, 8, 16])
            c1 = TT('c1', [128, 16, 8, 16]); c2 = TT('c2', [128, 16, 8, 16])
            Hbre = TT('Hbre', [128, 16, 2, 128]); Hbnim = TT('Hbnim', [128, 16, 2, 128])
            mhalf = TT('mhalf', [128, 2])
            S.asel(mhalf[:, 0:1], ones_f[:, 0:1], [[0, 1]], ALU.is_ge, 0.0, 63, -1, r=['ones_f'], w=['mhalf'])
            S.asel(mhalf[:, 1:2], ones_f[:, 0:1], [[0, 1]], ALU.is_ge, 0.0, -64, 1, r=['ones_f'], w=['mhalf'])

            def cmul_outer(dre, dim_, kre, kim, slot0, d, Tre_, Tim_, tre_k, tim_k, neg_im=False):
                c0 = d * 16
                pr = PWre[:, c0:c0 + 16, slot0:slot0 + 8].unsqueeze(3).to_broadcast([128, 16, 8, 16])
                pi = PWim[:, c0:c0 + 16, slot0:slot0 + 8].unsqueeze(3).to_broadcast([128, 16, 8, 16])
                tr_ = Tre_[:, c0:c0 + 16, :].unsqueeze(2).to_broadcast([128, 16, 8, 16])
                ti_ = Tim_[:, c0:c0 + 16, :].unsqueeze(2).to_broadcast([128, 16, 8, 16])
                S.tt('dve', c1[:], pr, tr_, ALU.mult, r=['PWre', tre_k], w=['c1', ('c1', 0), ('c1', 1)])
                S.tt('pool', c2[:], pi, ti_, ALU.mult, r=['PWim', tim_k], w=['c2'])
                S.tt('dve', dre, c1[:], c2[:], ALU.subtract, r=['c1', 'c2'], w=[kre])
                S.tt('dve', c1[:], pr, ti_, ALU.mult, r=['PWre', tim_k], w=['c1', ('c1', 0), ('c1', 1)])
                S.tt('pool', c2[:], pi, tr_, ALU.mult, r=['PWim', tre_k], w=['c2'])
                if neg_im:
                    S.stt(dim_, c1[:], -1.0, c2[:], ALU.mult, ALU.subtract, r=['c1', 'c2'], w=[kim])
                else:
                    S.tt('dve', dim_, c1[:], c2[:], ALU.add, r=['c1', 'c2'], w=[kim])

            ps.cfg([0, 1, 2, 3, 4, 5], [])
            for d in range(2):
                if d == 0:
                    cmul_outer(Gre[:], Gim[:], 'Gre', 'Gim', SL_NEG, 0, BBre, BBim, 'BBre', 'BBim')
                    S.mark('u1')
                    cmul_outer(G7re[:], G7im[:], 'G7re', 'G7im', SL_REV7, 0, BBre, BBim, 'BBre', 'BBim')
                    cmul_outer(Hre[:], Hnim[:], 'Hre', 'Hnim', SL_POS, 0, CTre, CTim, 'CTre', 'CTim', neg_im=True)
                    wre, wim, wkr, wki = G7re, G7im, 'G7re', 'G7im'
                    mslot = SL_POS1
                else:
                    cmul_outer(Gre[:], Gim[:], 'Gre', 'Gim', SL_POS, 1, BBre, BBim, 'BBre', 'BBim')
                    cmul_outer(Hre[:], Hnim[:], 'Hre', 'Hnim', SL_NEG, 1, CTre, CTim, 'CTre', 'CTim', neg_im=True)
                    wre, wim, wkr, wki = Gre, Gim, 'Gre', 'Gim'
                    mslot = SL_REV8
                cmul_outer(tabM[:, :, d, 0, :].rearrange("p g (j q) -> p g j q", q=16),
                           tabM[:, :, d, 1, :].rearrange("p g (j q) -> p g j q", q=16),
                           'tabM', 'tabM', mslot, d, CTre, CTim, 'CTre', 'CTim', neg_im=True)
                S.mark('u2')
                for two in range(2):
                    S.ts('dve', Hbre[:, :, two, :], Hre[:].rearrange("p g j q -> p g (j q)"), mhalf[:, two:two + 1], r=['Hre', 'mhalf'], w=['Hbre'])
                    S.ts('dve', Hbnim[:, :, two, :], Hnim[:].rearrange("p g j q -> p g (j q)"), mhalf[:, two:two + 1], r=['Hnim', 'mhalf'], w=['Hbnim'])
                for gp in range(16):
                    pk, pkk = ps.full()
                    S.mm(pk[:, 0:256], lhsT=Gre[:, gp, :, :], rhs=Hbre[:, gp, :, :], start=True, stop=False, r=['Gre', 'Hbre'], w=[pkk])
                    S.mm(pk[:, 0:256], lhsT=Gim[:, gp, :, :], rhs=Hbnim[:, gp, :, :], start=False, stop=True, r=['Gim', 'Hbnim'], w=[pkk])
                    for two in range(2):
                        g = 2 * gp + two
                        pkq = pk[:, two * 128:(two + 1) * 128]
                        if d == 0:
                            S.tt('dve', Kacc[:, g, :], pkq, mKf[:].rearrange("p j q -> p (j q)"), ALU.mult,
                                 r=[pkk, 'mKf'], w=[('Kacc', g)])
                        else:
                            S.tt('dve', c1[:, two, :, :].rearrange("p j q -> p (j q)"), pkq, mKb[:].rearrange("p j q -> p (j q)"),
                                 ALU.mult, r=[pkk, 'mKb'], w=[('c1', two)])
                            S.tt('pool', Kacc[:, g, :], Kacc[:, g, :], c1[:, two, :, :].rearrange("p j q -> p (j q)"), ALU.add,
                                 r=[('Kacc', g), ('c1', two)], w=[('Kacc', g)])
                            S.stt(tabK[:, g, :], ident_f[:], Dcol[:, g:g + 1], Kacc[:, g, :], ALU.mult, ALU.add,
                                  r=['ident_f', 'Dcol', ('Kacc', g)], w=['tabK'])
                    pw, pwk = ps.full()
                    S.tr(pw[:, 0:128], wre[:, gp, :, :], ident_f[:], r=[wkr, 'ident_f'], w=[pwk])
                    S.tr(pw[:, 128:256], wim[:, gp, :, :], ident_f[:], r=[wki, 'ident_f'], w=[pwk])
                    S.copy('act', tabW[:, 2 * gp:2 * gp + 2, d, 0, :], pw[:, 0:128].rearrange("p (t n) -> p t n", n=64), r=[pwk], w=['tabW'])
                    S.copy('act', tabW[:, 2 * gp:2 * gp + 2, d, 1, :], pw[:, 128:256].rearrange("p (t n) -> p t n", n=64), r=[pwk], w=['tabW'])
                S.mark('t4_%d' % d)
            S.dma('sp', tabK_d[:, :], tabK[:].rearrange("p g m -> p (g m)"), r=['tabK'], w=['tabK_d'])
            S.dma('sp', tabW_d[:, :], tabW[:].rearrange("p g d r n -> p (g d r n)"), r=['tabW'], w=['tabW_d'])
            S.dma('sp', tabM_d[:, :], tabM[:].rearrange("p g d r m -> p (g d r m)"), r=['tabM'], w=['tabM_d'])
            S.barrier()
            S.mark('tables')

        with ExitStack() as asx:
            adaw = T('adaw', [128, 8, 3 * D], F32, ctx=asx)
            cT = T('cT', [128, 8, NB], F32, ctx=asx)
            scT = T('scT', [128, 8, NB], F32, ctx=asx)
            adabT = T('adabT', [128, 24], F32, ctx=asx)
            normwT = T('normwT', [128, 8], F32, ctx=asx)
            adabg = T('adabg', [128, D], F32, ctx=asx)
            modT = T('modT', [128, 24, NB], F32, ctx=asx)
            for fc in range(8):
                S.dma('sp', adaw[:, fc, :], adaw_d[fc * 128:(fc + 1) * 128, :], w=[('adaw', fc)])
            S.dma('sp', cT[:], cT_d[:, :, :], w=['cT'])
            S.dma('sp', adabT[:], adabT_d[:, :], w=['adabT'])
            S.dma('sp', normwT[:], normwT_d[:, :], w=['normwT'])
            S.dma('sp', adabg[:], adabg_d[0:1, :].to_broadcast([128, D]), w=['adabg'])
            S.act(scT[:], cT[:], AF.Silu, r=['cT'], w=['scT'])
            ps.cfg([0, 1, 2, 3, 4, 5], [])
            pm, pmk = ps.full()
            for oc in range(24):
                for fc in range(8):
                    S.mm(pm[:, oc * NB:(oc + 1) * NB], lhsT=adaw[:, fc, oc * 128:(oc + 1) * 128], rhs=scT[:, fc, :],
                         start=(fc == 0), stop=(fc == 7), r=[('adaw', fc), 'scT'], w=[pmk])
            S.tt('dve', modT[:], pm[:, 0:24 * NB].rearrange("p (o b) -> p o b", b=NB),
                 adabT[:].unsqueeze(2).to_broadcast([128, 24, NB]), ALU.add, r=[pmk, 'adabT'], w=['modT'])
            S.copy('dve', shiftT[:], modT[:, 0:8, :], r=['modT'], w=['shiftT'])
            S.ts('dve', A_sc[:], modT[:, 8:16, :], 1.0, None, ALU.add, None, r=['modT'], w=['A_sc'])
            S.tt('dve', A_sc[:], A_sc[:], normwT[:].unsqueeze(2).to_broadcast([128, 8, NB]), ALU.mult,
                 r=['A_sc', 'normwT'], w=['A_sc'])
            for b in range(NB):
                for hf in range(2):
                    pg, pgk = ps.full()
                    for fc in range(8):
                        S.mm(pg[:, :], lhsT=scT[:, fc, b:b + 1].to_broadcast([128, 128]),
                             rhs=adaw[:, fc, 2 * D + hf * 512:2 * D + (hf + 1) * 512],
                             start=(fc == 0), stop=(fc == 7), r=[('adaw', fc), 'scT'], w=[pgk])
                    S.tt('dve', gate_row[:, b, hf * 512:(hf + 1) * 512], pg[:, :], adabg[:, hf * 512:(hf + 1) * 512], ALU.add,
                         r=[pgk, 'adabg'], w=[('gate_row', b)])
            S.barrier()
            S.mark('ada')

        hT = T('hT', [128, 8, L], BF16)

        eps_t = T('eps_t', [128, 1])
        lnq_t = T('lnq_t', [128, 1])
        S.memset('pool', eps_t[:], EPS, w=['eps_t'])
        S.memset('pool', lnq_t[:], -0.5 * math.log(128.0), w=['lnq_t'])

        for b in range(NB):
            ps.cfg([0, 1, 2, 3, 4, 5], [])
            with ExitStack() as px:
                sfx = '_p1_%d' % b
                xpool = Pool(S, px, 'xt' + sfx, [128, D], F32, 3)
                xspool = Pool(S, px, 'xs' + sfx, [128, D], BF16, 2)
                tmpool = Pool(S, px, 'tm' + sfx, [128, 8, 128], F32, 2)
                sspool = Pool(S, px, 'ss' + sfx, [128, 2], F32, 4)
                sq = T('sq' + sfx, [128, D], BF16, ctx=px)
                for t in range(NT):
                    xt, xk = xpool.next()
                    S.dma('sp', xt[:], x_d[b, t * 128:(t + 1) * 128, :], w=[xk])
                    ss, ssk = sspool.next()
                    S.memset('pool', ss[:], 0.0, w=[ssk])
                    S.act(sq[:], xt[:], AF.Square, accum=ss[:, 0:1], r=[xk], w=['sq', ssk])
                    S.act(ss[:, 1:2], ss[:, 0:1], AF.Ln, bias=eps_t[:, 0:1], scale=1.0 / D, r=[ssk, 'eps_t'], w=[ssk])
                    S.act(ss[:, 1:2], ss[:, 1:2], AF.Exp, scale=-0.5, r=[ssk], w=[ssk])
                    xs, xsk = xspool.next()
                    S.act(xs[:], xt[:], AF.Identity, scale=ss[:, 1:2], r=[xk, ssk], w=[xsk])
                    pb, pbk = ps.bfull()
                    for fc in range(8):
                        S.tr(pb[:, fc * 128:(fc + 1) * 128], xs[:, fc * 128:(fc + 1) * 128], ident_b[:], r=[xsk, 'ident_b'], w=[pbk])
                    tm, tmk = tmpool.next()
                    S.tt('dve', tm[:], pb[:].rearrange("p (f t) -> p f t", t=128),
                         A_sc[:, :, b].unsqueeze(2).to_broadcast([128, 8, 128]), ALU.mult, r=[pbk, 'A_sc'], w=[tmk])
                    S.tt('pool', hT[:, :, t * 128:(t + 1) * 128], tm[:],
                         shiftT[:, :, b].unsqueeze(2).to_broadcast([128, 8, 128]), ALU.add, r=[tmk, 'shiftT'], w=[('hT', t)])
                if dbg:
                    S.dma('sp', hTdbg_d[b], hT[:], r=[('hT', t) for t in range(NT)], w=['hTdbg'])
                S.barrier()
                S.mark('p1')

            ps.cfg([0, 1, 2, 3, 4, 5], [])
            with ExitStack() as sx:
                sfx = '_s5_%d' % b
                Wu = T('Wu' + sfx, [128, 8, 512], BF16, ctx=sx)
                Wza = T('Wza' + sfx, [128, 8, 512], BF16, ctx=sx)
                gluw = T('gluw' + sfx, [128, 4, 1024], BF16, ctx=sx)
                S.dma('pool', Wu[:], win_d[:, 0:512].rearrange("(fc p) c -> p fc c", p=128), w=['Wu'])
                S.dma('pool', Wza[:], win_d[:, 512:1024].rearrange("(fc p) c -> p fc c", p=128), w=['Wza'])
                S.dma('pool', gluw[:], gluw_d[:, :].rearrange("(kc p) o -> p kc o", p=128), w=['gluw'])
                UY = T('UY' + sfx, [128, 4, L], BF16, ctx=sx)
                UYu = UY[:].rearrange("p cc (g c) -> p cc g c", g=8)
                UTpool = Pool(S, sx, 'UT' + sfx, [128, 32, 8, 16], BF16, 1)
                tKp = Pool(S, sx, 'tK' + sfx, [128, 2, 128], BF16, 2)
                tWp = Pool(S, sx, 'tW' + sfx, [128, 2, 2, 2, 64], BF16, 2)
                tMp = Pool(S, sx, 'tM' + sfx, [128, 2, 2, 128], BF16, 2)
                PPt = [[[T('PP%d%d%d' % (d, ri, pp) + sfx, [128, NCK], F32, ctx=sx) for pp in range(2)] for ri in range(2)] for d in range(2)]
                Xbt = [[[T('Xb%d%d%d' % (d, ri, sl) + sfx, [128, NCK + 2], BF16, ctx=sx) for sl in range(2)] for ri in range(2)] for d in range(2)]
                for d in range(2):
                    for ri in range(2):
                        for sl in range(2):
                            S.memset('pool', Xbt[d][ri][sl][:], 0.0, w=[('Xb', d, ri, sl)])
                Ygp = Pool(S, sx, 'Yg' + sfx, [128, NCK], BF16, 2)
                YTp = Pool(S, sx, 'YT' + sfx, [128, NTB, 8, 128], BF16, 1)
                sgp = Pool(S, sx, 'sg' + sfx, [128, 512], F32, 2)
                szp = Pool(S, sx, 'sz' + sfx, [128, 512], F32, 2)
                yop = Pool(S, sx, 'yo' + sfx, [128, 512], BF16, 2)

                for tb in range(NTB):
                    UT, UTk = UTpool.next()
                    for j in range(8):
                        pp_, ppk = ps.full()
                        for fc in range(8):
                            S.mm(pp_[:, :], lhsT=hT[:, fc, tb * 1024 + j:(tb + 1) * 1024:8], rhs=Wu[:, fc, :],
                                 start=(fc == 0), stop=(fc == 7),
                                 r=[('hT', tb * 8 + q) for q in range(8)] + ['Wu'], w=[ppk])
                        S.copy('act' if j % 2 == 0 else 'dve', UT[:, :, j, :], pp_[:, :].rearrange("p (g q) -> p g q", q=16), r=[ppk], w=[UTk])
                    for g in range(32):
                        pq_, pqk = ps.bq()
                        S.tr(pq_, UT[:, g, :, :], ident_b[:], r=[UTk, 'ident_b'], w=[pqk])
                        S.copy('act' if g % 2 == 0 else 'dve', UYu[:, g // 8, g % 8, tb * 128:(tb + 1) * 128], pq_,
                               r=[pqk], w=[('U', g)] + [('yT', g // 8, q) for q in range(NTB)])

                YT = None
                for gp in range(16):
                    tK, tKk = tKp.next(); tW, tWk = tWp.next(); tM, tMk = tMp.next()
                    S.dma('sp', tK[:].rearrange("p t m -> p (t m)"), tabK_d[:, 2 * gp * 128:(2 * gp + 2) * 128], r=['tabK_d'], w=[tKk])
                    S.dma('sp', tW[:].rearrange("p t d r n -> p (t d r n)"), tabW_d[:, 2 * gp * 256:(2 * gp + 2) * 256], r=['tabW_d'], w=[tWk])
                    S.dma('sp', tM[:].rearrange("p d r m -> p (d r m)"), tabM_d[:, gp * 512:(gp + 1) * 512], r=['tabM_d'], w=[tMk])
                    sl = gp % 2
                    Ug = [UYu[:, (2 * gp + two) // 8, (2 * gp + two) % 8, :] for two in range(2)]
                    for d in range(2):
                        col = d * 16 + gp
                        for ri in range(2):
                            pS_, pSk = ps.full()
                            for two in range(2):
                                S.mm(pS_[64 * two:64 * two + 64, 0:NCK], lhsT=tW[:, two, d, ri, :], rhs=Ug[two],
                                     r=[tWk, ('U', 2 * gp + two)], w=[pSk])
                            S.copy('act', PPt[d][ri][0][:], pS_[:, 0:NCK], r=[pSk], w=[('PP', d, ri, 0)])
                        cur, oth = 0, 1
                        for k in range(NLV):
                            dd = 1 << k
                            last = (k == NLV - 1)
                            if d == 0:
                                dst = slice(dd, NCK); src = slice(0, NCK - dd); keep = slice(0, dd)
                                xdst = slice(1 + dd, 1 + NCK); xkeep = slice(1, 1 + dd)
                            else:
                                dst = slice(0, NCK - dd); src = slice(dd, NCK); keep = slice(NCK - dd, NCK)
                                xdst = slice(1, 1 + NCK - dd); xkeep = slice(1 + NCK - dd, 1 + NCK)
                            are = APre[:, col, k:k + 1]; aim = APim[:, col, k:k + 1]; naim = nAPim[:, col, k:k + 1]
                            cre, cim = PPt[d][0][cur], PPt[d][1][cur]
                            ore, oim = PPt[d][0][oth], PPt[d][1][oth]
                            kc_re, kc_im = ('PP', d, 0, cur), ('PP', d, 1, cur)
                            ko_re, ko_im = ('PP', d, 0, oth), ('PP', d, 1, oth)
                            S.stt(ore[:, dst], cim[:, src], naim, cre[:, dst], ALU.mult, ALU.add, r=[kc_re, kc_im, 'nAPim'], w=[ko_re])
                            S.stt(oim[:, dst], cre[:, src], aim, cim[:, dst], ALU.mult, ALU.add, r=[kc_re, kc_im, 'APim'], w=[ko_im])
                            if last:
                                S.stt(Xbt[d][0][sl][:, xdst], cre[:, src], are, ore[:, dst], ALU.mult, ALU.add,
                                      r=[kc_re, ko_re, 'APre'], w=[('Xb', d, 0, sl)])
                                S.stt(Xbt[d][1][sl][:, xdst], cim[:, src], are, oim[:, dst], ALU.mult, ALU.add,
                                      r=[kc_im, ko_im, 'APre'], w=[('Xb', d, 1, sl)])
                                S.copy('pool', Xbt[d][0][sl][:, xkeep], cre[:, keep], r=[kc_re], w=[('Xb', d, 0, sl)])
                                S.copy('pool', Xbt[d][1][sl][:, xkeep], cim[:, keep], r=[kc_im], w=[('Xb', d, 1, sl)])
                            else:
                                S.stt(ore[:, dst], cre[:, src], are, ore[:, dst], ALU.mult, ALU.add, r=[kc_re, ko_re, 'APre'], w=[ko_re])
                                S.stt(oim[:, dst], cim[:, src], are, oim[:, dst], ALU.mult, ALU.add, r=[kc_im, ko_im, 'APre'], w=[ko_im])
                                S.copy('pool', ore[:, keep], cre[:, keep], r=[kc_re], w=[ko_re])
                                S.copy('pool', oim[:, keep], cim[:, keep], r=[kc_im], w=[ko_im])
                            cur, oth = oth, cur
                    for two in range(2):
                        g = 2 * gp + two
                        cc = g // 8
                        if g % 8 == 0:
                            YT, YTk = YTp.next()
                        pY, pYk = ps.full()
                        S.mm(pY[:, 0:NCK], lhsT=tK[:, two, :], rhs=Ug[two], start=True, stop=False, r=[tKk, ('U', g)], w=[pYk])
                        for d in range(2):
                            off = 0 if d == 0 else 2
                            for ri in range(2):
                                S.mm(pY[:, 0:NCK], lhsT=tM[64 * two:64 * two + 64, d, ri, :],
                                     rhs=Xbt[d][ri][sl][64 * two:64 * two + 64, off:off + NCK],
                                     start=False, stop=(d == 1 and ri == 1), r=[tMk, ('Xb', d, ri, sl)], w=[pYk])
                        Yg, Ygk = Ygp.next()
                        S.act(Yg[:], pY[:, 0:NCK], AF.Gelu, r=[pYk], w=[Ygk])
                        pq_, pqk = ps.bfull()
                        for cb in range(NTB):
                            S.tr(pq_[:, cb * 128:(cb + 1) * 128], Yg[:, cb * 128:(cb + 1) * 128], ident_b[:], r=[Ygk, 'ident_b'], w=[pqk])
                        S.copy('dve', YT[:, :, :, (g % 8) * 16:(g % 8) * 16 + 16],
                               pq_[:, 0:NTB * 128].rearrange("p (c j q) -> p c j q", j=8, q=16), r=[pqk], w=[YTk])
                        if g % 8 == 7:
                            for tb in range(NTB):
                                pq_, pqk = ps.bfull()
                                for j in range(8):
                                    S.tr(pq_[:, j * 128:(j + 1) * 128], YT[:, tb, j, :], ident_b[:], r=[YTk, 'ident_b'], w=[pqk])
                                S.copy('act', UY[:, cc, tb * 1024:(tb + 1) * 1024].rearrange("p (c j) -> p c j", j=8),
                                       pq_[:].rearrange("p (j c) -> p c j", c=128), r=[pqk],
                                       w=[('yT', cc, tb)] + [('U', 8 * cc + q) for q in range(8)])

                for blk in range(NB5):
                    tsl = slice(blk * 512, (blk + 1) * 512)
                    for oc in range(4):
                        pa, pak = ps.full()
                        for kc in range(4):
                            S.mm(pa[:, :], lhsT=gluw[:, kc, oc * 128:(oc + 1) * 128], rhs=UY[:, kc, tsl], start=(kc == 0), stop=(kc == 3),
                                 r=['gluw', ('yT', kc, blk // 2)], w=[pak])
                        pg, pgk = ps.full()
                        for kc in range(4):
                            S.mm(pg[:, :], lhsT=gluw[:, kc, 512 + oc * 128:512 + (oc + 1) * 128], rhs=UY[:, kc, tsl], start=(kc == 0), stop=(kc == 3),
                                 r=['gluw', ('yT', kc, blk // 2)], w=[pgk])
                        pz, pzk = ps.full()
                        for fc in range(8):
                            S.mm(pz[:, :], lhsT=Wza[:, fc, oc * 128:(oc + 1) * 128], rhs=hT[:, fc, tsl], start=(fc == 0), stop=(fc == 7),
                                 r=['Wza'] + [('hT', blk * 4 + q) for q in range(4)], w=[pzk])
                        sg, sgk = sgp.next(); sz, szk = szp.next(); yo, yok = yop.next()
                        S.act(sg[:], pg[:, :], AF.Sigmoid, bias=glubT[:, 4 + oc:5 + oc], r=[pgk, 'glubT'], w=[sgk])
                        S.act(sz[:], pz[:, :], AF.Silu, r=[pzk], w=[szk])
                        S.stt(sg[:], pa[:, :], glubT[:, oc:oc + 1], sg[:], ALU.add, ALU.mult, r=[pak, 'glubT', sgk], w=[sgk])
                        S.tt('pool', yo[:], sg[:], sz[:], ALU.mult, r=[sgk, szk], w=[yok])
                        S.dma('sp', ycat_d[b, oc * 128:(oc + 1) * 128, tsl], yo[:], r=[yok], w=[('ycat', blk)], key=('ycat', blk))
                S.barrier()
                S.mark('s5')

            with ExitStack() as dx:
                sfx = '_dn_%d' % b
                pre = T('pre' + sfx, [128, 3, L + 4], BF16, ctx=dx)
                post = T('post' + sfx, [128, 3, L], BF16, ctx=dx)
                Oacc = T('Oacc' + sfx, [128, NT, 128], F32, ctx=dx)
                szb = T('szb' + sfx, [128, L], BF16, ctx=dx)
                Wqp = Pool(S, dx, 'Wq' + sfx, [128, 8, 3, 128], BF16, 1)
                Wzp = Pool(S, dx, 'Wz' + sfx, [128, 8, 128], BF16, 1)
                Wba = T('Wba' + sfx, [128, 8, 16], BF16, ctx=dx)
                diagw = T('diagw' + sfx, [128, 15, 128], BF16, ctx=dx)
                ba = T('ba' + sfx, [128, NT, 16], F32, ctx=dx)
                bsb = T('bsb' + sfx, [128, NT, 8], F32, ctx=dx)
                negb = T('negb' + sfx, [128, NT, 8], F32, ctx=dx)
                gsb = T('gsb' + sfx, [128, NT, 8], F32, ctx=dx)
                g1 = T('g1' + sfx, [128, NT, 8], F32, ctx=dx)
                g2 = T('g2' + sfx, [128, NT, 8], F32, ctx=dx)
                ssq = T('ssq' + sfx, [128, NT], F32, ctx=dx)
                rsq = T('rsq' + sfx, [128, NT], F32, ctx=dx)
                sqb = Pool(S, dx, 'sqb' + sfx, [128, 512], BF16, 2)
                lnp = Pool(S, dx, 'lnp' + sfx, [128, 512], F32, 2)
                f32p = Pool(S, dx, 'f32p' + sfx, [128, 128], F32, 12)
                b16p = Pool(S, dx, 'b16p' + sfx, [128, 128], BF16, 40)
                scp = Pool(S, dx, 'scp' + sfx, [128, 8], F32, 4)
                Sf = [T('Sf%d' % d + sfx, [128, 128], F32, ctx=dx) for d in range(2)]
                Sbf = [T('Sbf%d' % d + sfx, [128, 128], BF16, ctx=dx) for d in range(2)]
                ybp = Pool(S, dx, 'yb' + sfx, [128, 512], BF16, 2)
                junk2 = T('junk2' + sfx, [128, 128], BF16, ctx=dx)
                for cc3 in range(3):
                    S.memset('pool', pre[:, cc3, 0:2], 0.0, w=[('pre', cc3, 0)])
                    S.memset('pool', pre[:, cc3, L + 2:L + 4], 0.0, w=[('pre', cc3, NB5 - 1)])

                ps.cfg([0, 1], [2, 3, 4, 5])
                S.dma('pool', Wba[:], win_d[:, 3072:3088].rearrange("(fc p) c -> p fc c", p=128), w=['Wba'])
                pba, pbak = ps.full()
                for t in range(NT):
                    for fc in range(8):
                        S.mm(pba[:, t * 16:(t + 1) * 16], lhsT=hT[:, fc, t * 128:(t + 1) * 128], rhs=Wba[:, fc, :],
                             start=(fc == 0), stop=(fc == 7), r=[('hT', t), 'Wba'], w=[pbak])
                S.copy('dve', ba[:], pba[:, 0:NT * 16].rearrange("p (t c) -> p t c", c=16), r=[pbak], w=['ba'])
                S.act(bsb[:], ba[:, :, 0:8], AF.Sigmoid, r=['ba'], w=['bsb'])
                S.ts('dve', negb[:], bsb[:], -1.0, r=['bsb'], w=['negb'])
                S.tt('dve', g1[:], ba[:, :, 8:16], dtb_b[:].unsqueeze(1).to_broadcast([128, NT, 8]), ALU.add, r=['ba', 'dtb_b'], w=['g1'])
                S.stt(g2[:], g1[:], -1.0, g1[:], ALU.mult, ALU.max, r=['g1'], w=['g2'])
                S.act(g2[:], g2[:], AF.Exp, scale=-1.0, r=['g2'], w=['g2'])
                S.act(g2[:], g2[:], AF.Ln, bias=1.0, r=['g2'], w=['g2'])
                S.ts('dve', g1[:], g1[:], 0.0, None, ALU.max, None, r=['g1'], w=['g1'])
                S.tt('dve', g1[:], g1[:], g2[:], ALU.add, r=['g1', 'g2'], w=['g1'])
                S.tt('dve', gsb[:], g1[:], nexpa[:].unsqueeze(1).to_broadcast([128, NT, 8]), ALU.mult, r=['g1', 'nexpa'], w=['gsb'])
                S.mark('d0')

                for h in range(4):
                    Wq, Wqk = Wqp.next(); Wz, Wzk = Wzp.next()
                    for cc3 in range(3):
                        c0 = 1024 + cc3 * 512 + h * 128
                        S.dma('pool', Wq[:, :, cc3, :], win_d[:, c0:c0 + 128].rearrange("(fc p) c -> p fc c", p=128), w=[Wqk])
                    S.dma('pool', Wz[:], win_d[:, 2560 + h * 128:2560 + (h + 1) * 128].rearrange("(fc p) c -> p fc c", p=128), w=[Wzk])
                    for cc3 in range(3):
                        for j in range(5):
                            S.ts('pool', diagw[:, cc3 * 5 + j, :], ident_f[:], convw[:, cc3 * 4 + h, j:j + 1], r=['ident_f', 'convw'], w=['diagw'])
                    for cc3 in range(3):
                        for blk in range(NB5):
                            pp_, ppk = ps.full()
                            for fc in range(8):
                                S.mm(pp_[:, :], lhsT=Wq[:, fc, cc3, :], rhs=hT[:, fc, blk * 512:(blk + 1) * 512], start=(fc == 0), stop=(fc == 7),
                                     r=[Wqk] + [('hT', blk * 4 + q) for q in range(4)], w=[ppk])
                            S.copy('act' if blk % 2 == 0 else 'dve', pre[:, cc3, 2 + blk * 512:2 + (blk + 1) * 512], pp_[:, :], r=[ppk], w=[('pre', cc3, blk)])
                    for blk in range(NB5):
                        pp_, ppk = ps.full()
                        for fc in range(8):
                            S.mm(pp_[:, :], lhsT=Wz[:, fc, :], rhs=hT[:, fc, blk * 512:(blk + 1) * 512], start=(fc == 0), stop=(fc == 7),
                                 r=[Wzk] + [('hT', blk * 4 + q) for q in range(4)], w=[ppk])
                        S.act(szb[:, blk * 512:(blk + 1) * 512], pp_[:, :], AF.Silu, r=[ppk], w=[('szb', blk)])
                    for cc3 in range(3):
                        for blk in range(NB5):
                            pp_, ppk = ps.full()
                            for j in range(5):
                                S.mm(pp_[:, :], lhsT=diagw[:, cc3 * 5 + j, :], rhs=pre[:, cc3, blk * 512 + j:blk * 512 + j + 512], start=(j == 0), stop=(j == 4),
                                     r=['diagw'] + [('pre', cc3, q) for q in range(max(0, blk - 1), min(NB5, blk + 2))], w=[ppk])
                            S.act(post[:, cc3, blk * 512:(blk + 1) * 512], pp_[:, :], AF.Silu, r=[ppk], w=[('post', cc3, blk)])
                    for cc3 in range(2):
                        for blk in range(NB5):
                            sqt, sqk = sqb.next(); lnt, lnk = lnp.next()
                            pslc = post[:, cc3, blk * 512:(blk + 1) * 512]
                            S.act(sqt[:], pslc, AF.Square, r=[('post', cc3, blk)], w=[sqk])
                            pp_, ppk = ps.full()
                            S.mm(pp_[:, :], lhsT=ones_b[:], rhs=sqt[:], r=['ones_b', sqk], w=[ppk])
                            S.act(lnt[:], pp_[:, :], AF.Ln, bias=eps_t[:, 0:1], r=[ppk, 'eps_t'], w=[lnk])
                            if cc3 == 0:
                                S.act(lnt[:], lnt[:], AF.Exp, scale=-0.5, bias=lnq_t[:, 0:1], r=[lnk, 'lnq_t'], w=[lnk])
                            else:
                                S.act(lnt[:], lnt[:], AF.Exp, scale=-0.5, r=[lnk], w=[lnk])
                            S.tt('dve', pslc, pslc, lnt[:], ALU.mult, r=[('post', cc3, blk), lnk], w=[('post', cc3, blk)])
                    S.mark('d1')

                    for d in range(2):
                        S.memset('pool', Sf[d][:], 0.0, w=[('Sf', d)])
                        S.memset('pool', Sbf[d][:], 0.0, w=[('Sbf', d)])
                    visited = set()
                    for s_ in range(NT):
                        for d in range(2):
                            n = s_ if d == 0 else NT - 1 - s_
                            blk = n // 4
                            tok = slice(n * 128, (n + 1) * 128)
                            hd = d * 4 + h
                            qT_c = post[:, 0, tok]; kT_c = post[:, 1, tok]; vT_c = post[:, 2, tok]
                            kq, kk_, kvv = ('post', 0, blk), ('post', 1, blk), ('post', 2, blk)
                            gcol = gsb[:, n, hd:hd + 1]; bcol = bsb[:, n, hd:hd + 1]; nbcol = negb[:, n, hd:hd + 1]
                            tri, trik = (triF, 'triF') if d == 0 else (triB, 'triB')
                            lastc = 127 if d == 0 else 0
                            if d == 0:
                                mpat, mcm = [[1, 128]], -1
                            else:
                                mpat, mcm = [[-1, 128]], 1
                            pG, pGk = ps.quarter()
                            S.mm(pG, lhsT=gcol.to_broadcast([128, 128]), rhs=tri[:], r=['gsb', trik], w=[pGk])
                            pc, pck = ps.quarter()
                            S.mm(pc[:, 0:1], lhsT=tri[:], rhs=gcol, r=['gsb', trik], w=[pck])
                            sc, sck = scp.next()
                            S.act(sc[:, 0:1], pc[:, 0:1], AF.Identity, scale=-1.0, r=[pck], w=[sck])
                            S.act(sc[:, 1:2], pG[:, lastc:lastc + 1], AF.Identity, r=[pGk], w=[sck])
                            S.act(sc[:, 2:3], pc[:, 0:1], AF.Exp, r=[pck], w=[sck])
                            S.act(sc[:, 3:4], pc[:, 0:1], AF.Exp, scale=-1.0, bias=sc[:, 1:2], r=[pck, sck], w=[sck])
                            S.act(sc[:, 4:5], pG[:, lastc:lastc + 1], AF.Exp, r=[pGk], w=[sck])
                            S.mark('d2')
                            D1, D1k = f32p.next()
                            S.ts('dve', D1[:], pG, sc[:, 0:1], 0.0, ALU.add, ALU.min, r=[pGk, sck], w=[D1k])
                            D1m, D1mk = f32p.next()
                            S.act(D1m[:], D1[:], AF.Exp, r=[D1k], w=[D1mk])
                            GTi, GTik = f32p.next()
                            S.asel(GTi[:], D1m[:], mpat, ALU.is_ge, 0.0, 0, mcm, r=[D1mk], w=[GTik])
                            GTs, GTsk = f32p.next()
                            S.asel(GTs[:], GTi[:], mpat, ALU.is_gt, 0.0, 0, mcm, r=[GTik], w=[GTsk])
                            EG, EGk = f32p.next()
                            S.act(EG[:], pG, AF.Exp, r=[pGk], w=[EGk])
                            pKK, pKKk = ps.quarter()
                            S.mm(pKK, lhsT=kT_c, rhs=kT_c, r=[kk_], w=[pKKk])
                            pKQ, pKQk = ps.quarter()
                            S.mm(pKQ, lhsT=kT_c, rhs=qT_c, r=[kk_, kq], w=[pKQk])
                            U0, U0k = b16p.next()
                            S.stt(U0[:], pKK, bcol, GTs[:], ALU.mult, ALU.mult, r=[pKKk, 'bsb', GTsk], w=[U0k])
                            AT, ATk = b16p.next()
                            S.tt('dve', AT[:], pKQ, GTi[:], ALU.mult, r=[pKQk, GTik], w=[ATk])
                            qd, qdk = b16p.next()
                            S.tt('pool', qd[:], qT_c, EG[:], ALU.mult, r=[kq, EGk], w=[qdk])
                            pT_, pTk = ps.bq()
                            S.tr(pT_, kT_c, ident_b[:], r=[kk_, 'ident_b'], w=[pTk])
                            ktl, ktlk = b16p.next()
                            S.act(ktl[:], pT_, AF.Identity, scale=sc[:, 3:4], r=[pTk, sck], w=[ktlk])
                            pV_, pVk = ps.bq()
                            S.tr(pV_, vT_c, ident_b[:], r=[kvv, 'ident_b'], w=[pVk])
                            Vt, Vtk = b16p.next()
                            S.copy('act', Vt[:], pV_, r=[pVk], w=[Vtk])
                            pU_, pUk = ps.bq()
                            S.tr(pU_, U0[:], ident_b[:], r=[U0k, 'ident_b'], w=[pUk])
                            U0T, U0Tk = b16p.next()
                            S.copy('dve', U0T[:], pU_, r=[pUk], w=[U0Tk])
                            S.mark('d3')
                            tmpm, tmpmk = b16p.next()
                            S.tt('pool', tmpm[:], U0[:], lmask[:, 0, :], ALU.mult, r=[U0k, 'lmask'], w=[tmpmk])
                            Yt, Yk = b16p.next()
                            S.tt('pool', Yt[:], ident_b[:], tmpm[:], ALU.subtract, r=['ident_b', tmpmk], w=[Yk])
                            tmpm2, tmpm2k = b16p.next()
                            S.tt('pool', tmpm2[:], U0T[:], lmask[:, 0, :], ALU.mult, r=[U0Tk, 'lmask'], w=[tmpm2k])
                            YT_, YTk_ = b16p.next()
                            S.tt('pool', YT_[:], ident_b[:], tmpm2[:], ALU.subtract, r=['ident_b', tmpm2k], w=[YTk_])
                            for lv in range(1, 7):
                                p1, p1k = ps.quarter()
                                S.mm(p1, lhsT=U0T[:], rhs=Yt[:], r=[U0Tk, Yk], w=[p1k])
                                T1, T1k = b16p.next()
                                S.tt('dve', T1[:], p1, lmask[:, lv, :], ALU.mult, r=[p1k, 'lmask'], w=[T1k])
                                p2, p2k = ps.quarter()
                                S.mm(p2, lhsT=YT_[:], rhs=T1[:], r=[YTk_, T1k], w=[p2k])
                                Yn, Ynk = b16p.next()
                                S.tt('dve', Yn[:], Yt[:], p2, ALU.subtract, r=[Yk, p2k], w=[Ynk])
                                Yt, Yk = Yn, Ynk
                                if lv < 6:
                                    p3, p3k = ps.bq()
                                    S.tr(p3, Yt[:], ident_b[:], r=[Yk, 'ident_b'], w=[p3k])
                                    YTn, YTnk = b16p.next()
                                    S.copy('act', YTn[:], p3, r=[p3k], w=[YTnk])
                                    YT_, YTk_ = YTn, YTnk
                            NTt, NTk = Yt, Yk
                            pKS, pKSk = ps.quarter()
                            S.mm(pKS, lhsT=kT_c, rhs=Sbf[d][:], r=[kk_, ('Sbf', d)], w=[pKSk])
                            nR, nRk = b16p.next()
                            S.stt(nR[:], pKS, sc[:, 2:3], Vt[:], ALU.mult, ALU.subtract, r=[pKSk, sck, Vtk], w=[nRk])
                            pNR, pNRk = ps.quarter()
                            S.mm(pNR, lhsT=NTt[:], rhs=nR[:], r=[NTk, nRk], w=[pNRk])
                            vn, vnk = b16p.next()
                            S.act(vn[:], pNR, AF.Identity, scale=nbcol, r=[pNRk, 'negb'], w=[vnk])
                            S.mark('e2')
                            pO, pOk = ps.quarter()
                            S.mm(pO, lhsT=qd[:], rhs=Sbf[d][:], start=True, stop=False, r=[qdk, ('Sbf', d)], w=[pOk])
                            S.mm(pO, lhsT=AT[:], rhs=vn[:], start=False, stop=True, r=[ATk, vnk], w=[pOk])
                            pSn, pSnk = ps.quarter()
                            S.mm(pSn, lhsT=ktl[:], rhs=vn[:], r=[ktlk, vnk], w=[pSnk])
                            S.stt(Sf[d][:], Sf[d][:], sc[:, 4:5], pSn, ALU.mult, ALU.add, r=[('Sf', d), sck, pSnk], w=[('Sf', d)])
                            S.copy('act', Sbf[d][:], Sf[d][:], r=[('Sf', d)], w=[('Sbf', d)])
                            S.mark('e3')
                            if n not in visited:
                                visited.add(n)
                                S.copy('act', Oacc[:, n, :], pO, r=[pOk], w=[('Oacc', n)])
                            else:
                                S.tt('dve', Oacc[:, n, :], pO, Oacc[:, n, :], ALU.add, r=[pOk, ('Oacc', n)], w=[('Oacc', n)])
                            S.mark('d4')
                            S.mark('c_%d_%d_%d' % (h, s_, d))

                    S.mark('h%d' % h)
                    S.memset('pool', ssq[:], 0.0, w=['ssq'])
                    for n in range(NT):
                        S.act(junk2[:], Oacc[:, n, :], AF.Square, accum=ssq[:, n:n + 1], r=[('Oacc', n)], w=['junk2', 'ssq'])
                    S.act(rsq[:], ssq[:], AF.Ln, bias=eps_t[:, 0:1], scale=1.0 / 128.0, r=['ssq', 'eps_t'], w=['rsq'])
                    S.act(rsq[:], rsq[:], AF.Exp, scale=-0.5, r=['rsq'], w=['rsq'])
                    yb = None
                    for n in range(NT):
                        on, onk = b16p.next()
                        S.act(on[:], Oacc[:, n, :], AF.Identity, scale=rsq[:, n:n + 1], r=[('Oacc', n), 'rsq'], w=[onk])
                        pT_, pTk = ps.bq()
                        S.tr(pT_, on[:], ident_b[:], r=[onk, 'ident_b'], w=[pTk])
                        if n % 4 == 0:
                            yb, ybk = ybp.next()
                        S.stt(yb[:, (n % 4) * 128:(n % 4 + 1) * 128], pT_, dnw[:, 0:1], szb[:, n * 128:(n + 1) * 128], ALU.mult, ALU.mult,
                              r=[pTk, 'dnw', ('szb', n // 4)], w=[ybk])
                        if n % 4 == 3:
                            blk = n // 4
                            S.dma('sp', ycat_d[b, 512 + h * 128:512 + (h + 1) * 128, blk * 512:(blk + 1) * 512], yb[:],
                                  r=[ybk], w=[('ycat', blk)], key=('ycat', blk))
                S.barrier()
                S.mark('dn')

            ps.cfg([0, 1, 2, 3, 4, 5], [])
            with ExitStack() as fx:
                sfx = '_fin_%d' % b
                woutg = T('woutg' + sfx, [128, 8, D], BF16, ctx=fx)
                wstp = Pool(S, fx, 'wst' + sfx, [128, D], F32, 2)
                ycp = Pool(S, fx, 'ycp' + sfx, [128, 8, 512], BF16, 2)
                xpool = Pool(S, fx, 'xf' + sfx, [128, D], F32, 3)
                rp = Pool(S, fx, 'rp' + sfx, [128, D], F32, 2)
                op_ = Pool(S, fx, 'osb' + sfx, [128, D], F32, 2)
                sspool = Pool(S, fx, 'ssf' + sfx, [128, 2], F32, 4)
                sq = T('sqf' + sfx, [128, D], BF16, ctx=fx)
                for kc in range(8):
                    wst, wstk = wstp.next()
                    S.dma('sp', wst[:], wout_d[kc * 128:(kc + 1) * 128, :], w=[wstk])
                    S.tt('pool', woutg[:, kc, :], wst[:], gate_row[:, b, :], ALU.mult, r=[wstk, ('gate_row', b)], w=[('woutg', kc)])
                S.mark('f0')
                ycv = ycat_d[b].rearrange("(kc p) t -> p kc t", p=128)
                for blk in range(NB5):
                    yc, yck = ycp.next()
                    S.dma('sp', yc[:], ycv[:, :, blk * 512:(blk + 1) * 512], r=[('ycat', blk)], w=[yck])
                    for t4 in range(4):
                        t = blk * 4 + t4
                        xt, xk = xpool.next()
                        S.dma('sp', xt[:], x_d[b, t * 128:(t + 1) * 128, :], w=[xk])
                        rr, rk = rp.next()
                        for hf in range(2):
                            po, pok = ps.full()
                            for kc in range(8):
                                S.mm(po[:, :], lhsT=yc[:, kc, t4 * 128:(t4 + 1) * 128], rhs=woutg[:, kc, hf * 512:(hf + 1) * 512],
                                     start=(kc == 0), stop=(kc == 7), r=[yck, ('woutg', kc)], w=[pok])
                            S.tt('dve', rr[:, hf * 512:(hf + 1) * 512], po[:, :], xt[:, hf * 512:(hf + 1) * 512], ALU.add, r=[pok, xk], w=[rk])
                        S.mark('f1')
                        ss, ssk = sspool.next()
                        S.memset('pool', ss[:], 0.0, w=[ssk])
                        S.act(sq[:], rr[:], AF.Square, accum=ss[:, 0:1], r=[rk], w=['sqf', ssk])
                        S.act(ss[:, 1:2], ss[:, 0:1], AF.Ln, bias=eps_t[:, 0:1], scale=1.0 / D, r=[ssk, 'eps_t'], w=[ssk])
                        S.act(ss[:, 1:2], ss[:, 1:2], AF.Exp, scale=-0.5, r=[ssk], w=[ssk])
                        ot, otk = op_.next()
                        S.stt(ot[:], rr[:], ss[:, 1:2], fnw_row[:], ALU.mult, ALU.mult, r=[rk, ssk, 'fnw_row'], w=[otk])
                        S.dma('sp', out_d[b, t * 128:(t + 1) * 128, :], ot[:], r=[otk], w=[('out', t % 2)], key=('out', t % 2))
                        S.mark('f2')
                S.barrier()

        S.barrier()
    return nc


def _prep_shared(inp):
    f32 = np.float32
    g = lambda k: np.asarray(inp[k], dtype=f32)[0]
    sh = {}
    sh["ada_w"] = np.ascontiguousarray(g("ada_w"))
    ada_b = g("ada_b")
    sh["ada_bT"] = np.ascontiguousarray(ada_b.reshape(24, 128).T)
    sh["ada_bg"] = np.ascontiguousarray(ada_b[2 * D:3 * D].reshape(1, D))
    sh["norm_wT"] = np.ascontiguousarray(g("norm_w").reshape(8, 128).T)
    sh["w_in"] = np.ascontiguousarray(g("w_in"))
    sh["conv_wT"] = np.ascontiguousarray(g("conv_w").reshape(5, 12, 128).transpose(2, 1, 0))
    sh["dn_alog"] = np.concatenate([g("dn_a_log_f"), g("dn_a_log_b")]).reshape(1, 8).astype(f32)
    sh["dn_dtb"] = np.concatenate([g("dn_dt_bias_f"), g("dn_dt_bias_b")]).reshape(1, 8).astype(f32)
    sh["dn_norm_w"] = np.ascontiguousarray(g("dn_norm_w").reshape(128, 1))

    def n2(a):
        a = a.reshape((16, 2, 64) + a.shape[2:])
        a = np.moveaxis(a, 0, 2)
        return np.ascontiguousarray(a.reshape((128, 16) + a.shape[3:]))
    lamre, lamim, lstep, Bre, Bim, CTre, CTim = [], [], [], [], [], [], []
    for d in ("f", "b"):
        lamre.append(n2(g("lam_re_" + d)))
        lamim.append(n2(g("lam_im_" + d)))
        ls = np.repeat(g("log_step_" + d)[:, None], 64, axis=1)
        lstep.append(n2(ls))
        Bre.append(n2(g("b_re_" + d)))
        Bim.append(n2(g("b_im_" + d)))
        CTre.append(n2(g("c_re_" + d).transpose(0, 2, 1)))
        CTim.append(n2(g("c_im_" + d).transpose(0, 2, 1)))
    cat = lambda l: np.ascontiguousarray(np.concatenate(l, axis=1).astype(f32))
    sh["lamre"], sh["lamim"], sh["lstep"] = cat(lamre), cat(lamim), cat(lstep)
    sh["Bre"], sh["Bim"], sh["CTre"], sh["CTim"] = cat(Bre), cat(Bim), cat(CTre), cat(CTim)
    s5d = g("s5_d").reshape(32, 16)
    sh["Dcol"] = np.ascontiguousarray(np.tile(s5d.T[None, :, :], (8, 1, 1)).reshape(128, 32))
    sh["glu_w"] = np.ascontiguousarray(g("glu_w"))
    sh["glu_bT"] = np.ascontiguousarray(g("glu_b").reshape(8, 128).T)
    sh["w_out"] = np.ascontiguousarray(g("w_out"))
    sh["fnw"] = np.ascontiguousarray(np.asarray(inp["final_norm_w"], dtype=f32).reshape(1, D))
    return sh


def _core_map(sh, xs, cs):
    m = dict(sh)
    m["x"] = np.ascontiguousarray(xs, dtype=np.float32)
    nb = cs.shape[0]
    m["cT"] = np.ascontiguousarray(np.asarray(cs, dtype=np.float32).reshape(nb, 8, 128).transpose(2, 1, 0))
    return m


def kernel(**inputs):
    x = np.asarray(inputs["x"], dtype=np.float32)
    c = np.asarray(inputs["c"], dtype=np.float32)
    B, L, _ = x.shape
    NB = B // NCORES
    sh = _prep_shared(inputs)
    nc = build_nc(L, NB)
    in_maps = [_core_map(sh, x[i * NB:(i + 1) * NB], c[i * NB:(i + 1) * NB]) for i in range(NCORES)]
    res = run_bass_kernel_spmd(nc, in_maps, core_ids=list(range(NCORES)))
    return np.concatenate([np.asarray(r["out"]) for r in res.results], axis=0).astype(np.float32)
```
